# Optimizing a Trainium2 kernel written in Bass

```python
import jax, jax.numpy as jnp
from jax import lax
import numpy as np

D_MODEL = 1024
BATCH = 8
SEQ = 4096
DEPTH = 1

CHUNK = 64
EPS = 1e-6
N_HEADS_A = 8
HEAD_DIM_A = 64
D_A = N_HEADS_A * HEAD_DIM_A
N_PREV_CHUNKS = 8
BAND = (N_PREV_CHUNKS + 1) * CHUNK
REL_CLIP = 128
N_REL = 2 * REL_CLIP + 1
SGU_CHUNK = 128
N_GROUPS_B = 4
GROUP_DIM_B = 128
D_B = N_GROUPS_B * GROUP_DIM_B
SPLITS = (D_A, D_A, D_A, D_A, D_B, D_B, D_B, D_MODEL, D_MODEL)
D_IN = sum(SPLITS)
NEG_INF = -1e30

kernel_name = 'hybrid_chunked_attn_gmlp_gated'


def rmsnorm(x, g):
    xf = x.astype(jnp.float32)
    y = xf * lax.rsqrt(jnp.mean(xf * xf, axis=-1, keepdims=True) + EPS)
    return (y * g.astype(jnp.float32)).astype(x.dtype)


def layernorm(x, g, b):
    xf = x.astype(jnp.float32)
    mu = jnp.mean(xf, axis=-1, keepdims=True)
    var = jnp.mean(jnp.square(xf - mu), axis=-1, keepdims=True)
    y = (xf - mu) * lax.rsqrt(var + EPS)
    return (y * g.astype(jnp.float32) + b.astype(jnp.float32)).astype(x.dtype)


def chunked_rel_attention(q, k, v, rel_bias):
    b, s, _ = q.shape
    nc = s // CHUNK
    qc = q.reshape(b, nc, CHUNK, N_HEADS_A, HEAD_DIM_A)

    def band(t):
        t = t.reshape(b, nc, CHUNK, N_HEADS_A, HEAD_DIM_A)
        tp = jnp.pad(t, ((0, 0), (N_PREV_CHUNKS, 0), (0, 0), (0, 0), (0, 0)))
        return jnp.concatenate([tp[:, j:j + nc] for j in range(N_PREV_CHUNKS + 1)], axis=2)

    kb, vb = band(k), band(v)
    q_off = jnp.arange(CHUNK)
    k_off = jnp.arange(BAND) - N_PREV_CHUNKS * CHUNK
    dist = q_off[:, None] - k_off[None, :]
    bias = rel_bias[:, jnp.clip(dist, -REL_CLIP, REL_CLIP) + REL_CLIP].astype(jnp.float32)
    key_chunk = jnp.arange(nc)[:, None] + k_off[None, :] // CHUNK
    valid = key_chunk >= 0
    scale = HEAD_DIM_A ** -0.5
    scores = jnp.einsum('bnqhd,bnkhd->bhnqk', qc, kb).astype(jnp.float32) * scale
    scores = scores + bias[None, :, None, :, :]
    scores = jnp.where(valid[None, None, :, None, :], scores, NEG_INF)
    p = jax.nn.softmax(scores, axis=-1).astype(v.dtype)
    out = jnp.einsum('bhnqk,bnkhd->bnqhd', p, vb)
    return out.reshape(b, s, D_A)


def spatial_gating(u, v, ln_g, ln_b, w_s, b_s):
    b, s, _ = v.shape
    nb = s // SGU_CHUNK
    vn = layernorm(v, ln_g, ln_b).reshape(b, nb, SGU_CHUNK, N_GROUPS_B, GROUP_DIM_B)
    tri = jnp.tril(jnp.ones((SGU_CHUNK, SGU_CHUNK), dtype=bool))
    w = jnp.where(tri[None], w_s, jnp.zeros_like(w_s))
    mixed = jnp.einsum('gts,bnsgc->bntgc', w, vn) + jnp.transpose(b_s)[:, :, None]
    return u * mixed.reshape(b, s, D_B)


def hybrid_layer(x, norm_g, w_in, b_gate, rel_bias, sgu_ln_g, sgu_ln_b, w_s, b_s, w_pa, w_pb, w_out):
    h = rmsnorm(x, norm_g)
    z = jnp.einsum('bsd,de->bse', h, w_in)
    idx = list(np.cumsum(SPLITS)[:-1])
    q, k, v, g_a, u_b, v_b, g_b, gate_a, gate_b = jnp.split(z, idx, axis=-1)
    y_a = chunked_rel_attention(q, k, v, rel_bias) * jax.nn.silu(g_a)
    y_b = spatial_gating(jax.nn.gelu(u_b), jax.nn.gelu(v_b), sgu_ln_g, sgu_ln_b, w_s, b_s) * jax.nn.silu(g_b)
    p_a = jnp.einsum('bse,ed->bsd', y_a, w_pa)
    p_b = jnp.einsum('bse,ed->bsd', y_b, w_pb)
    ga = jax.nn.sigmoid(gate_a + b_gate[:D_MODEL])
    gb = jax.nn.sigmoid(gate_b + b_gate[D_MODEL:])
    merged = ga * p_a + gb * p_b
    return x + jnp.einsum('bsd,de->bse', merged, w_out)


def setup_inputs(seed: int = 0) -> dict:
    key = jax.random.key(seed)
    ks = jax.random.split(key, 16)
    f32 = jnp.float32
    nrm = lambda k, shape, s: jax.random.normal(k, shape, f32) * s
    return {
        'x': jax.random.normal(ks[0], (BATCH, SEQ, D_MODEL), f32),
        'norm_g': 1.0 + nrm(ks[1], (DEPTH, D_MODEL), 0.05),
        'w_in': nrm(ks[2], (DEPTH, D_MODEL, D_IN), D_MODEL ** -0.5),
        'b_gate': nrm(ks[3], (DEPTH, 2 * D_MODEL), 0.1),
        'rel_bias': nrm(ks[4], (DEPTH, N_HEADS_A, N_REL), 0.5),
        'sgu_ln_g': 1.0 + nrm(ks[5], (DEPTH, D_B), 0.05),
        'sgu_ln_b': nrm(ks[6], (DEPTH, D_B), 0.05),
        'w_s': nrm(ks[7], (DEPTH, N_GROUPS_B, SGU_CHUNK, SGU_CHUNK), SGU_CHUNK ** -0.5),
        'b_s': 1.0 + nrm(ks[8], (DEPTH, N_GROUPS_B, SGU_CHUNK), 0.1),
        'w_pa': nrm(ks[9], (DEPTH, D_A, D_MODEL), D_A ** -0.5),
        'w_pb': nrm(ks[10], (DEPTH, D_B, D_MODEL), D_B ** -0.5),
        'w_out': nrm(ks[11], (DEPTH, D_MODEL, D_MODEL), D_MODEL ** -0.5),
        'final_g': 1.0 + nrm(ks[12], (D_MODEL,), 0.05),
    }


def reference(x, norm_g, w_in, b_gate, rel_bias, sgu_ln_g, sgu_ln_b, w_s, b_s, w_pa, w_pb, w_out, final_g):
    for l in range(DEPTH):
        x = hybrid_layer(x, norm_g[l], w_in[l], b_gate[l], rel_bias[l], sgu_ln_g[l], sgu_ln_b[l],
                         w_s[l], b_s[l], w_pa[l], w_pb[l], w_out[l])
    return rmsnorm(x, final_g)
```

```python
import numpy as np
import concourse.bass as bass
import concourse.mybir as mybir
from concourse.bass_utils import run_bass_kernel_spmd

F32 = mybir.dt.float32
BF16 = mybir.dt.bfloat16
ALU = mybir.AluOpType
AF = mybir.ActivationFunctionType

D = 1024
S = 4096
DIN = 5632
NB = 16
BT = 256
EPS = 1e-6
C_Q, C_K, C_V, C_GA, C_UB, C_VB, C_GB, C_GTA, C_GTB = 0, 512, 1024, 1536, 2048, 2560, 3072, 3584, 4608
GELU_C = 0.7978845608028654
GELU_A = 0.044715


class Res:
    __slots__ = ("name", "w", "rs", "psum")

    def __init__(self, name, psum=False):
        self.name = name
        self.w = None
        self.rs = {}
        self.psum = psum


class DmaSem:
    def __init__(self, nc, name):
        self.sem = nc.alloc_semaphore(name)
        self.cnt = 0


class Eng:
    def __init__(self, nc, h, name):
        self.h = h
        self.name = name
        self.sem = nc.alloc_semaphore("s_" + name)
        self.cnt = 0
        self.waited = {}


class Sched:
    def __init__(self, nc):
        self.nc = nc
        self.pe = Eng(nc, nc.tensor, "pe")
        self.act = Eng(nc, nc.scalar, "act")
        self.dve = Eng(nc, nc.vector, "dve")
        self.pool = Eng(nc, nc.gpsimd, "pool")
        self.sp = Eng(nc, nc.sync, "sp")
        self.engs = [self.pe, self.act, self.dve, self.pool, self.sp]
        self.nwait = 0
        self.clock = {}

    def _wait(self, e, sem, val):
        if e.waited.get(sem.name, 0) >= val:
            return
        e.h.wait_ge(sem, val)
        e.waited[sem.name] = val
        self.nwait += 1
        for k, v in self.clock.get((sem.name, val), {}).items():
            if e.waited.get(k, 0) < v:
                e.waited[k] = v

    def _snap(self, e, ev):
        c = dict(e.waited)
        if e is not self.sp:
            c[e.sem.name] = e.cnt
        self.clock[(ev[0].name, ev[1])] = c

    def deps(self, e, reads, writes):
        for r in reads:
            if r.w is not None:
                sem, val = r.w
                if not (sem is e.sem and e is self.pe):
                    self._wait(e, sem, val)
            if r.psum:
                for sem, val in r.rs.values():
                    if sem is not e.sem:
                        self._wait(e, sem, val)
        for w in writes:
            if w.w is not None:
                sem, val = w.w
                if not (sem is e.sem and e is self.pe):
                    self._wait(e, sem, val)
            for sem, val in w.rs.values():
                if not (sem is e.sem and e is self.pe):
                    self._wait(e, sem, val)

    def _record(self, ev, reads, writes):
        for r in reads:
            r.rs[ev[0].name] = ev
        for w in writes:
            w.w = ev
            w.rs = {}

    def op(self, e, fn, reads=(), writes=()):
        self.deps(e, reads, writes)
        ins = fn()
        e.cnt += 1
        ins.then_inc(e.sem, 1)
        self._snap(e, (e.sem, e.cnt))
        self._record((e.sem, e.cnt), reads, writes)

    def group(self, e, fns, reads=(), writes=()):
        self.deps(e, reads, writes)
        ins = None
        for fn in fns:
            ins = fn()
        e.cnt += 1
        ins.then_inc(e.sem, 1)
        self._snap(e, (e.sem, e.cnt))
        self._record((e.sem, e.cnt), reads, writes)

    def dma(self, e, ds, out, in_, reads=(), writes=(), **kw):
        self.deps(e, reads, writes)
        ins = e.h.dma_start(out=out, in_=in_, **kw)
        ds.cnt += 1
        ins.then_inc(ds.sem, 16)
        self._snap(e, (ds.sem, 16 * ds.cnt))
        self._record((ds.sem, 16 * ds.cnt), reads, writes)

    def barrier(self):
        for e in self.engs:
            for o in self.engs:
                if o is not e and o.cnt > 0:
                    self._wait(e, o.sem, o.cnt)


def build_nc(nblocks=NB, dbg=False):
    nc = bass.Bass("TRN2", target_bir_lowering=False)
    dt = nc.dram_tensor
    x_d = dt("x", [S, D], F32, kind="ExternalInput").ap()
    ng_d = dt("norm_g", [D], F32, kind="ExternalInput").ap()
    win_d = dt("w_in", [D, DIN], F32, kind="ExternalInput").ap()
    bg_d = dt("b_gate", [2 * D], F32, kind="ExternalInput").ap()
    rb_d = dt("rel_bias", [8, 257], F32, kind="ExternalInput").ap()
    lng_d = dt("sgu_ln_g", [512], F32, kind="ExternalInput").ap()
    lnb_d = dt("sgu_ln_b", [512], F32, kind="ExternalInput").ap()
    ws_d = dt("w_s", [4, 128, 128], F32, kind="ExternalInput").ap()
    bs_d = dt("b_s", [512], F32, kind="ExternalInput").ap()
    wpa_d = dt("w_pa", [512, D], F32, kind="ExternalInput").ap()
    wpb_d = dt("w_pb", [512, D], F32, kind="ExternalInput").ap()
    wout_d = dt("w_out", [D, D], F32, kind="ExternalInput").ap()
    fg_d = dt("final_g", [D], F32, kind="ExternalInput").ap()
    id_d = dt("ident", [128, 128], F32, kind="ExternalInput").ap()
    tr_d = dt("trilT", [128, 128], F32, kind="ExternalInput").ap()
    y_d = dt("y", [S, D], F32, kind="ExternalOutput").ap()
    ext_d = dt("ext_scr", [8, 384], F32).ap()
    t2_d = dt("toe_scr", [8, 128, 256], F32).ap()

    def sb(name, shape, dtype):
        return nc.alloc_sbuf_tensor(name, shape, dtype)

    def dsz(dtype):
        return 4 if dtype == F32 else 2

    W = sb("W", [128, 8, DIN], BF16)
    Wpa = sb("Wpa", [128, 4, D], BF16)
    Wpb = sb("Wpb", [128, 4, D], BF16)
    Wout = sb("Wout", [128, 8, D], BF16)
    NF = 3
    Fs = [sb(f"F{i}", [128, D], F32) for i in range(NF)]
    hb = [sb(f"hb{i}", [128, D], BF16) for i in range(2)]
    hT = [sb(f"hT{i}", [128, 8, BT], BF16) for i in range(2)]
    qT = sb("qT", [128, 4, BT], BF16)
    kT = sb("kT", [128, 4, 768], BF16)
    Vr = sb("Vr", [128, 6, 768], BF16)
    PT = [sb("PT0", [128, 2, 512], BF16)]
    pt0 = nc.sbuf_base - 2048
    PT += [sb(f"PT{i}", [128, 2, 512], BF16) for i in (1, 2)]
    NT1 = 4
    T1 = [sb(f"T1_{i}", [128, BT], F32) for i in range(NT1)]
    vf = sb("vf", [128, 512], F32)
    vsq = sb("vsq", [128, 512], F32)
    vn = [sb(f"vn{i}", [128, 512], BF16) for i in range(2)]
    reg0 = nc.sbuf_base
    merged = sb("merged", [128, 8, BT], BF16)
    reg0 = nc.sbuf_base - 8 * BT * 2
    Gt = sb("Gt", [128, 2, BT], F32)
    M1 = sb("M1", [128, 2, BT], F32)
    ya0 = sb("ya0", [128, 4, BT], BF16)
    yb0 = sb("yb0", [128, 4, BT], BF16)
    assert nc.sbuf_base - reg0 == 12288, (nc.sbuf_base, reg0)
    _off = [reg0]

    def sbat(name, shape, dtype):
        n = int(np.prod(shape[1:])) * dsz(dtype)
        t = nc.alloc_sbuf_tensor_at(name, shape, dtype, offset=_off[0])
        _off[0] += (n + 31) // 32 * 32
        assert _off[0] <= reg0 + 12288
        return t

    ident_f = sbat("ident_f", [128, 128], F32)
    prow = sbat("prow", [28, 128], F32)
    tril_f = sbat("tril_f", [128, 128], F32)
    ws_f = sbat("ws_f", [128, 4, 128], F32)
    ws_b = sbat("ws_b", [128, 4, 128], BF16)
    bsb = sbat("bsb", [128, 512], F32)
    lnb_bc = sbat("lnb_bc", [128, 512], F32)
    lnb_hi = sbat("lnb_hi", [128, 512], BF16)
    lnb_lo = sbat("lnb_lo", [128, 512], BF16)
    ya = [ya0, sb("ya1", [128, 4, BT], BF16)]
    yb = [yb0, sb("yb1", [128, 4, BT], BF16)]
    XA = [nc.alloc_sbuf_tensor_at(f"XA{i}", [128, D], F32, offset=pt0 + 4096 * i) for i in range(2)]
    XB = [nc.alloc_sbuf_tensor_at(f"XB{i}", [128, D], F32, offset=reg0 + 4096 * i) for i in range(2)]
    fgb = sb("fgb", [128, D], F32)
    Cg = sb("Cg", [128, 4, 128], F32)
    EBX = sb("EBX", [128, 8, 256], BF16)
    WT = sb("WT", [128, 4, 128], BF16)
    identb = sb("identb", [128, 128], BF16)
    ONES3 = sb("ONES3", [128, 192], BF16)
    ng = sb("ng", [128, 8], F32)
    bg = sb("bg", [128, 16], F32)
    hbg = sb("hbg", [128, 16], F32)
    lng = sb("lng", [128, 4], F32)
    ss = sb("ss", [128, 2], F32)
    sa2 = sb("sa2", [128, 2], F32)
    rstd = sb("rstd", [128, 2], F32)
    ss2 = sb("ss2", [128, 2], F32)
    sb2 = sb("sb2", [128, 2], F32)
    rstd2 = sb("rstd2", [128, 2], F32)
    s1 = sb("s1", [128, 2], F32)
    s2 = sb("s2", [128, 2], F32)
    mu = sb("mu", [128, 2], F32)
    msq = sb("msq", [128, 2], F32)
    var = sb("var", [128, 2], F32)
    rstdv = sb("rstdv", [128, 2], F32)
    rbs = sb("rbs", [8, 257], F32)
    es = sb("es", [8, 384], F32)
    nbias = sb("nbias", [8, 1], F32)
    epsc = sb("epsc", [128, 2], F32)

    NG = 8
    Db = [nc.alloc_psum_tensor(f"Db{i}", [128, 2, 512], F32) for i in range(4)]
    Gb = [Db[i // 2][:, i % 2, :] for i in range(NG)]

    sc = Sched(nc)
    pe, act, dve, pool, sp = sc.pe, sc.act, sc.dve, sc.pool, sc.sp
    R = Res
    Gres = [R(f"G{i}", True) for i in range(NG)]
    Fres = [R(f"F{i}") for i in range(NF)]
    Fd = [DmaSem(nc, f"d_F{i}") for i in range(NF)]
    XAres = [R(f"XA{i}") for i in range(2)]
    XBres = [R(f"XB{i}") for i in range(2)]
    XAd = [DmaSem(nc, f"d_XA{i}") for i in range(2)]
    XBd = [DmaSem(nc, f"d_XB{i}") for i in range(2)]
    setup_d = DmaSem(nc, "d_setup")
    chain_d = DmaSem(nc, "d_chain")
    cnt = {"g": 0, "f": 0, "t1": 0, "s": 0, "nd": 0, "cast": 0}

    pinned = set()

    def gnext(pin=False):
        for _ in range(NG):
            i = cnt["g"] % NG
            cnt["g"] += 1
            if i not in pinned:
                if pin:
                    pinned.add(i)
                return Gb[i], Gres[i]
        raise AssertionError("all generic PSUM banks pinned")

    def gunpin(res):
        pinned.discard(Gres.index(res))

    def snext():
        for _ in range(NG):
            i = cnt["g"] % NG
            if i % 2 == 1 or i in pinned or (i + 1) in pinned:
                cnt["g"] += 1
                continue
            cnt["g"] += 2
            return Db[i // 2], [Gres[i], Gres[i + 1]]
        raise AssertionError("no free PSUM bank pair")

    def fnext():
        i = cnt["f"] % NF
        cnt["f"] += 1
        return Fs[i], Fres[i], Fd[i]

    T1res = [R(f"T1_{i}") for i in range(NT1)]

    def t1next():
        i = cnt["t1"] % NT1
        cnt["t1"] += 1
        return T1[i], T1res[i]

    mm = nc.tensor.matmul

    def bcast_mid(a, n):
        pat = [list(p) for p in a.ap]
        return bass.AP(a.tensor, a.offset, [pat[0], [0, n]] + pat[1:])

    r_setup = R("setup_in")

    def sdma(out, in_, **kw):
        sc.dma(sp, setup_d, out, in_, reads=(), writes=(r_setup,), **kw)

    sdma(ident_f[:], id_d[:, :])
    sdma(tril_f[:], tr_d[:, :])
    sdma(ws_f[:], ws_d.rearrange("g t s -> t g s"))
    sdma(bsb[:], bs_d.partition_broadcast(128))
    sdma(lnb_bc[:], lnb_d.partition_broadcast(128))
    sdma(fgb[:], fg_d.partition_broadcast(128))
    sdma(prow[0:8, :], ng_d.rearrange("(r p) -> r p", p=128))
    sdma(prow[8:24, :], bg_d.rearrange("(r p) -> r p", p=128))
    sdma(prow[24:28, :], lng_d.rearrange("(r p) -> r p", p=128))
    sdma(rbs[:], rb_d[:, :])
    r_setup.w = (setup_d.sem, 16 * setup_d.cnt)

    r_c = {k: R(k) for k in ["identb", "hbg", "ws_b", "WT", "lnb_hi", "lnb_lo", "Cg", "nb", "es", "ext", "t2",
                             "EBX", "ONES3", "Vr0", "epsc"]}
    sc.op(pool, lambda: nc.gpsimd.memset(epsc[:, 0:1], EPS), [], [r_c["epsc"]])
    sc.op(pool, lambda: nc.gpsimd.memset(epsc[:, 1:2], 4.0 * EPS), [r_c["epsc"]], [r_c["epsc"]])
    sc.op(dve, lambda: nc.vector.tensor_copy(out=identb[:], in_=ident_f[:]), [r_setup], [r_c["identb"]])
    gb, gr = gnext()
    sc.group(pe, [lambda: mm(gb[:, 0:28], prow[0:28, :], ident_f[0:28, 0:28], start=True, stop=True)], [r_setup], [gr])
    r_c["pcols"] = R("pcols")
    sc.op(dve, lambda: nc.vector.tensor_copy(out=ng[:], in_=gb[:, 0:8]), [gr], [r_c["pcols"]])
    sc.op(dve, lambda: nc.vector.tensor_copy(out=lng[:], in_=gb[:, 24:28]), [gr], [r_c["pcols"]])
    sc.op(dve, lambda: nc.vector.tensor_scalar(out=hbg[:], in0=gb[:, 8:24], scalar1=0.5, scalar2=None, op0=ALU.mult),
          [gr], [r_c["hbg"]])
    sc.op(act, lambda: nc.scalar.copy(out=ws_b[:], in_=ws_f[:]), [r_setup], [r_c["ws_b"]])
    gb, gr = gnext()
    gbv = gb[:].bitcast(BF16).rearrange("p (a b) -> p a b", a=8)
    sc.group(pe, [(lambda g=g: nc.tensor.transpose(out=gbv[:, g, :], in_=ws_b[:, g, :], identity=identb[:]))
                  for g in range(4)], [r_c["ws_b"], r_c["identb"]], [gr])
    sc.op(dve, lambda: nc.vector.tensor_tensor(out=WT[:], in0=gbv[:, 0:4, :], in1=bcast_mid(tril_f[:], 4), op=ALU.mult),
          [gr, r_setup], [r_c["WT"]])
    sc.op(dve, lambda: nc.vector.tensor_copy(out=lnb_hi[:], in_=lnb_bc[:]), [r_setup], [r_c["lnb_hi"]])
    sc.op(dve, lambda: nc.vector.tensor_tensor(out=lnb_lo[:], in0=lnb_bc[:], in1=lnb_hi[:], op=ALU.subtract),
          [r_setup, r_c["lnb_hi"]], [r_c["lnb_lo"]])
    gb, gr = gnext()
    gcv = gb[:].rearrange("p (a b) -> p a b", a=4)
    fns = []
    for g in range(4):
        fns.append(lambda g=g: mm(gcv[:, g, :], lnb_hi[:, g * 128:(g + 1) * 128], WT[:, g, :], start=True, stop=False))
        fns.append(lambda g=g: mm(gcv[:, g, :], lnb_lo[:, g * 128:(g + 1) * 128], WT[:, g, :], start=False, stop=True))
    sc.group(pe, fns, [r_c["lnb_hi"], r_c["lnb_lo"], r_c["WT"]], [gr])
    sc.op(dve, lambda: nc.vector.tensor_tensor(out=Cg[:], in0=gcv[:, :, :],
                                               in1=bsb[:].rearrange("p (a b) -> p a b", a=4), op=ALU.add),
          [gr, r_setup], [r_c["Cg"]])
    sc.op(dve, lambda: nc.vector.tensor_scalar(out=nbias[:], in0=rbs[:, 256:257], scalar1=-1.0, scalar2=None,
                                               op0=ALU.mult), [r_setup], [r_c["nb"]])
    sc.op(pool, lambda: nc.gpsimd.memset(es[:], 1.0), [], [r_c["es"]])
    sc.op(act, lambda: nc.scalar.activation(out=es[:, 0:257], in_=rbs[:], func=AF.Exp, bias=nbias[:, 0:1], scale=1.0),
          [r_setup, r_c["nb"], r_c["es"]], [r_c["es"]])
    sc.dma(sp, chain_d, ext_d[:, :], es[:], [r_c["es"]], [r_c["ext"]])
    sc.dma(sp, chain_d, t2_d[:, :, :], bass.AP(ext_d.tensor, 128, [[384, 8], [-1, 128], [1, 256]]),
           [r_c["ext"]], [r_c["t2"]])
    for half in range(2):
        ft, fr, fd = fnext()
        sc.dma(sp, fd, ft[:].rearrange("p (a b) -> p a b", a=4),
               t2_d[4 * half:4 * half + 4, :, :].rearrange("h k j -> k h j"), [r_c["t2"]], [fr])
        sc.op(dve, lambda ft=ft, half=half: nc.vector.tensor_copy(
            out=EBX[:, 4 * half:4 * half + 4, 0:256], in_=ft[:].rearrange("p (a b) -> p a b", a=4)),
            [fr], [r_c["EBX"]])
    sc.op(pool, lambda: nc.gpsimd.memset(EBX[64:128, :, 0:64], 0.0), [r_c["EBX"]], [r_c["EBX"]])
    sc.op(pool, lambda: nc.gpsimd.memset(ONES3[:], 0.0), [], [r_c["ONES3"]])
    sc.op(pool, lambda: nc.gpsimd.memset(ONES3[:, 64:128], 1.0), [r_c["ONES3"]], [r_c["ONES3"]])
    sc.op(pool, lambda: nc.gpsimd.memset(Vr[:], 0.0), [], [r_c["Vr0"]])
    sc.barrier()

    Wres = {}
    wjobs = []
    for p in range(6):
        c0 = 1024 * p
        c1 = min(c0 + 1024, DIN)
        for kc in range(8):
            wjobs.append(("in", p, kc, win_d[kc * 128:(kc + 1) * 128, c0:c1], W[:, kc, c0:c1], c1 - c0, ng[:, kc:kc + 1]))
    for kc in range(4):
        wjobs.append(("pa", 0, kc, wpa_d[kc * 128:(kc + 1) * 128, :], Wpa[:, kc, :], D, 0.5))
    for kc in range(4):
        wjobs.append(("pb", 0, kc, wpb_d[kc * 128:(kc + 1) * 128, :], Wpb[:, kc, :], D, 0.25))
    for kc in range(8):
        wjobs.append(("out", 0, kc, wout_d[kc * 128:(kc + 1) * 128, :], Wout[:, kc, :], D, 0.5))

    wpool = {"slots": None, "i": 0}

    def set_wpool(kind):
        base = [(Fs[i], Fres[i], Fd[i]) for i in range(NF)]
        xa = [(XA[i], XAres[i], XAd[i]) for i in range(2)]
        xb = [(XB[i], XBres[i], XBd[i]) for i in range(2)]
        wpool["slots"] = {"all": base + xa + xb, "late": base + xb, "base": base}[kind]

    def fence(ress):
        for e in (pe, act, dve, pool):
            sc.deps(e, [], ress)

    def emit_wjob(job):
        name, p, kc, src, dst, n, scal = job
        ft, fr, fd = wpool["slots"][wpool["i"] % len(wpool["slots"])]
        wpool["i"] += 1
        sc.dma(sp, fd, ft[:, 0:n], src, [], [fr])
        res = R(f"W{name}{p}_{kc}")
        Wres[(name, p, kc)] = res
        k = cnt["cast"]
        cnt["cast"] += 1
        if k % 2 == 0:
            sc.op(act, lambda: nc.scalar.mul(out=dst, in_=ft[:, 0:n], mul=scal), [fr, r_setup], [res])
        else:
            sc.op(dve, lambda: nc.vector.tensor_scalar(out=dst, in0=ft[:, 0:n], scalar1=scal, scalar2=None, op0=ALU.mult),
                  [fr, r_setup], [res])

    def wr_in(c0, n=128):
        p0, p1 = c0 // 1024, (c0 + n - 1) // 1024
        return [Wres[("in", p, kc)] for p in range(p0, p1 + 1) for kc in range(8)]

    r_hb = [R("hb0"), R("hb1")]
    r_junk = R("junk")
    r_ss, r_sa2, r_rstd = R("ss"), R("sa2"), R("rstd")
    r_hT = [[R(f"hT{p}_{i}") for i in range(2)] for p in range(2)]
    r_q = [R(f"q{j}") for j in range(4)]
    r_k = [[R(f"k{s}_{j}") for j in range(4)] for s in range(3)]
    r_V = [R(f"V{s}") for s in range(6)]
    r_PT = [R(f"PT{u}") for u in range(3)]
    r_vf, r_vsq = R("vf"), R("vsq")
    r_vn = [R("vn0"), R("vn1")]
    r_s1, r_s2, r_mu, r_msq, r_var, r_rstdv = R("s1"), R("s2"), R("mu"), R("msq"), R("var"), R("rstdv")
    r_ya = [[R(f"ya{p}_{j}") for j in range(4)] for p in range(2)]
    r_yb = [[R(f"yb{p}_{g}") for g in range(4)] for p in range(2)]
    r_Gt, r_M1 = R("Gt"), R("M1")
    r_m = [R(f"m{j}") for j in range(8)]
    r_ss2, r_sb2, r_rstd2 = R("ss2"), R("sb2"), R("rstd2")
    xin = {}

    def normA(b):
        for i in range(2):
            ti = 2 * b + i
            ft, fr, fd = fnext()
            sc.dma(sp, fd, ft[:], x_d[ti * 128:(ti + 1) * 128, :], [], [fr])
            xin[ti] = (ft, fr)
            sc.op(act, lambda ft=ft, i=i: nc.scalar.activation(out=hb[i][:], in_=ft[:], func=AF.Square,
                                                              accum_out=ss[:, i:i + 1]),
                  [fr], [r_hb[i], r_ss])
        yield
        sc.op(act, lambda: nc.scalar.activation(out=sa2[:], in_=ss[:], func=AF.Sqrt, scale=1.0 / D, bias=epsc[:, 0:1]),
              [r_ss, r_c["epsc"]], [r_sa2])
        yield
        sc.op(dve, lambda: nc.vector.reciprocal(out=rstd[:], in_=sa2[:]), [r_sa2], [r_rstd])
        yield
        for i in range(2):
            ft, fr = xin[2 * b + i]
            sc.op(act, lambda ft=ft, i=i: nc.scalar.mul(out=hb[i][:], in_=ft[:], mul=rstd[:, i:i + 1]),
                  [fr, r_rstd], [r_hb[i]])
        yield

    def trans(b):
        for i in range(2):
            for half in range(2):
                gb, gr = gnext()
                gv = gb[:].rearrange("p (a b) -> p a b", a=4)
                sc.group(pe, [(lambda c=c, gv=gv: mm(gv[:, c, :], hb[i][:, (4 * half + c) * 128:(4 * half + c + 1) * 128],
                                                     identb[:], start=True, stop=True)) for c in range(4)],
                         [r_hb[i]], [gr])
                if half == 0:
                    sc.op(dve, lambda gv=gv: nc.vector.tensor_copy(out=hT[b % 2][:, 0:4, i * 128:(i + 1) * 128], in_=gv[:, :, :]),
                          [gr], [r_hT[b % 2][i]])
                else:
                    sc.op(act, lambda gv=gv: nc.scalar.copy(out=hT[b % 2][:, 4:8, i * 128:(i + 1) * 128], in_=gv[:, :, :]),
                          [gr], [r_hT[b % 2][i]])
                yield

    def proj_fm(b, out_ap, c0):
        return [(lambda kc=kc: mm(out_ap, W[:, kc, c0:c0 + 128], hT[b % 2][:, kc, :], start=(kc == 0), stop=(kc == 7)))
                for kc in range(8)]

    def projB(b, vb=(0, 1), qk=(0, 1, 2, 3), v=(0, 1)):
        seg = b % 3
        for i in vb:
            gb, gr = gnext(pin=True)
            sc.group(pe, [(lambda kc=kc, gb=gb, i=i: mm(gb[:, :], hT[b % 2][:, kc, i * 128:(i + 1) * 128], W[:, kc, C_VB:C_VB + 512],
                                                         start=(kc == 0), stop=(kc == 7))) for kc in range(8)],
                     [r_hT[b % 2][i]] + wr_in(C_VB, 512), [gr])
            sc.op(act, lambda gb=gb: nc.scalar.activation(out=vsq[:], in_=gb[:, :], func=AF.Square, scale=GELU_A ** 0.5),
                  [gr], [r_vsq])
            yield
            sc.op(dve, lambda gb=gb: nc.vector.scalar_tensor_tensor(out=vsq[:], in0=vsq[:], scalar=1.0, in1=gb[:, :],
                                                                    op0=ALU.add, op1=ALU.mult), [gr, r_vsq], [r_vsq])
            sc.op(act, lambda: nc.scalar.activation(out=vsq[:], in_=vsq[:], func=AF.Tanh, scale=GELU_C), [r_vsq], [r_vsq])
            yield
            sc.op(dve, lambda gb=gb: nc.vector.scalar_tensor_tensor(out=vf[:], in0=vsq[:], scalar=1.0, in1=gb[:, :],
                                                                    op0=ALU.add, op1=ALU.mult), [gr, r_vsq], [r_vf])
            gunpin(gr)
            sc.op(dve, lambda: nc.vector.tensor_reduce(out=s1[:, 0:1], in_=vf[:], axis=mybir.AxisListType.X, op=ALU.add),
                  [r_vf], [r_s1])
            sc.op(act, lambda: nc.scalar.activation(out=vsq[:], in_=vf[:], func=AF.Square, accum_out=s2[:, 0:1]),
                  [r_vf, r_vsq], [r_vsq, r_s2])
            yield
            sc.op(dve, lambda: nc.vector.tensor_scalar(out=mu[:, 0:1], in0=s1[:, 0:1], scalar1=1.0 / 512, scalar2=None,
                                                       op0=ALU.mult), [r_s1], [r_mu])
            sc.op(dve, lambda: nc.vector.scalar_tensor_tensor(out=msq[:, 0:1], in0=s1[:, 0:1], scalar=-1.0 / (512.0 * 512.0),
                                                              in1=s1[:, 0:1], op0=ALU.mult, op1=ALU.mult), [r_s1], [r_msq])
            sc.op(dve, lambda: nc.vector.scalar_tensor_tensor(out=var[:, 0:1], in0=s2[:, 0:1], scalar=1.0 / 512,
                                                              in1=msq[:, 0:1], op0=ALU.mult, op1=ALU.add),
                  [r_s2, r_msq], [r_var])
            sc.op(act, lambda: nc.scalar.activation(out=var[:, 0:1], in_=var[:, 0:1], func=AF.Sqrt, scale=1.0,
                                                    bias=epsc[:, 1:2]), [r_var, r_c["epsc"]], [r_var])
            yield
            sc.op(dve, lambda: nc.vector.reciprocal(out=rstdv[:, 0:1], in_=var[:, 0:1]), [r_var], [r_rstdv])
            sc.op(dve, lambda i=i: nc.vector.tensor_scalar(out=vn[i][:], in0=vf[:], scalar1=mu[:, 0:1],
                                                           scalar2=rstdv[:, 0:1], op0=ALU.subtract, op1=ALU.mult),
                  [r_vf, r_mu, r_rstdv], [r_vn[i]])
            yield
        for j in qk:
            gb, gr = gnext()
            gv = gb[:].rearrange("p (a b) -> p a b", a=2)
            sc.group(pe, proj_fm(b, gv[:, 0, :], C_Q + j * 128) + proj_fm(b, gv[:, 1, :], C_K + j * 128),
                     r_hT[b % 2] + wr_in(C_Q + j * 128) + wr_in(C_K + j * 128), [gr])
            sc.op(act, lambda gv=gv, j=j: nc.scalar.copy(out=qT[:, j, :], in_=gv[:, 0, :]), [gr], [r_q[j]])
            sc.op(dve, lambda gv=gv, j=j: nc.vector.tensor_copy(out=kT[:, j, seg * 256:(seg + 1) * 256], in_=gv[:, 1, :]),
                  [gr], [r_k[seg][j]])
            yield
        for i in v:
            ti = 2 * b + i
            slot = ti % 6
            gb, gr = gnext()
            sc.group(pe, [(lambda kc=kc, gb=gb, i=i: mm(gb[:, :], hT[b % 2][:, kc, i * 128:(i + 1) * 128], W[:, kc, C_V:C_V + 512],
                                                         start=(kc == 0), stop=(kc == 7))) for kc in range(8)],
                     [r_hT[b % 2][i]] + wr_in(C_V, 512), [gr])
            vdst = Vr[:, slot, :].rearrange("p (j c) -> p j c", j=4)
            sc.op(act, lambda gb=gb, vdst=vdst: nc.scalar.copy(
                out=vdst[:, :, 0:64], in_=gb[:, :].rearrange("p (j c) -> p j c", j=4)[:, :, 0:64]), [gr], [r_V[slot]])
            sc.op(dve, lambda gb=gb, vdst=vdst: nc.vector.tensor_copy(
                out=vdst[:, :, 128:192], in_=gb[:, :].rearrange("p (j c) -> p j c", j=4)[:, :, 64:128]), [gr], [r_V[slot]])
            yield
    TILES = {0: (0, 256, 0, 128), 1: (0, 0, 0, 256), 2: (1, 0, 0, 256), 3: (1, 256, 0, 256),
             4: (2, 0, 0, 256), 5: (2, 256, 128, 256)}
    UNIT_TILES = {0: (1, 0), 1: (2, 3), 2: (4, 5)}
    UNIT_W = {0: 384, 1: 512, 2: 384}

    def attention(b):
        units = [u for u in (2, 1, 0) if 2 * b - 4 + UNIT_TILES[u][0] >= 0 and 2 * b - 4 + UNIT_TILES[u][1] >= 0]
        items = [(j, u) for j in range(4) for u in units]
        tgs = {}
        nd = {}

        def nd_banks(j):
            if j not in nd:
                nb_, nr_ = gnext(pin=True)
                db_, dr_ = gnext(pin=True)
                nd[j] = (nb_, nr_, db_, dr_)
            return nd[j]

        def gate(j):
            gb, gr = gnext()
            sc.group(pe, proj_fm(b, gb[:, 0:256], C_GA + j * 128), r_hT[b % 2] + wr_in(C_GA + j * 128), [gr])
            tg, tgr = t1next()
            sc.op(act, lambda: nc.scalar.activation(out=tg[:], in_=gb[:, 0:256], func=AF.Tanh, scale=0.5), [gr], [tgr])
            sc.op(dve, lambda: nc.vector.scalar_tensor_tensor(out=tg[:], in0=tg[:], scalar=1.0, in1=gb[:, 0:256],
                                                              op0=ALU.add, op1=ALU.mult), [gr, tgr], [tgr])
            tgs[j] = (tg, tgr)

        def scores(j, u):
            sbuf, sres2 = snext()
            fns = []
            rd = [r_q[j]]
            for t in UNIT_TILES[u]:
                _, uoff, qlo, qhi = TILES[t]
                gt = 2 * b - 4 + t
                kpos = (gt * 128) % 768
                rd.append(r_k[kpos // 256][j])
                for r in range(2):
                    fns.append(lambda r=r, uoff=uoff, qlo=qlo, qhi=qhi, kpos=kpos: mm(
                        sbuf[:, r, uoff:uoff + qhi - qlo], kT[64 * r:64 * r + 64, j, kpos:kpos + 128],
                        qT[64 * r:64 * r + 64, j, qlo:qhi], start=True, stop=True))
            sc.group(pe, fns, rd, sres2)
            w = UNIT_W[u]
            sc.op(act, lambda: nc.scalar.activation(out=PT[u][:, :, 0:w], in_=sbuf[:, :, 0:w], func=AF.Exp, scale=0.125),
                  sres2, [r_PT[u]])
            if u == 0:
                sc.op(pool, lambda: nc.gpsimd.memset(PT[0][0:64, :, 320:384], 0.0), [r_PT[0]], [r_PT[0]])
                sc.op(pool, lambda: nc.gpsimd.memset(PT[0][0:64, :, 192:256], 0.0), [r_PT[0]], [r_PT[0]])
            elif u == 1:
                sc.op(dve, lambda: nc.vector.tensor_tensor(out=PT[1][:, :, 256:384], in0=PT[1][:, :, 256:384],
                                                           in1=EBX[:, 2 * j:2 * j + 2, 128:256], op=ALU.mult),
                      [r_PT[1]], [r_PT[1]])
            else:
                sc.op(dve, lambda: nc.vector.tensor_tensor(out=PT[2][:, :, 0:256], in0=PT[2][:, :, 0:256],
                                                           in1=EBX[:, 2 * j:2 * j + 2, 0:256], op=ALU.mult),
                      [r_PT[2]], [r_PT[2]])
                sc.op(pool, lambda: nc.gpsimd.tensor_tensor(out=PT[2][:, :, 256:384], in0=PT[2][:, :, 256:384],
                                                            in1=EBX[:, 2 * j:2 * j + 2, 0:128], op=ALU.mult),
                      [r_PT[2]], [r_PT[2]])

        def pv(j, u):
            numb, numr, denb, denr = nd_banks(j)
            first = (u == units[0])
            last = (u == units[-1])
            tiles = sorted(UNIT_TILES[u])
            fns = []
            n = 2 * len(tiles)
            k = 0
            for t in tiles:
                _, uoff, qlo, qhi = TILES[t]
                slot = (2 * b - 4 + t) % 6
                for r in range(2):
                    st = first and k == 0
                    sp_ = last and k == n - 1
                    lhs_v = Vr[:, slot, 192 * j + 64 * r:192 * j + 64 * r + 128]
                    lhs_1 = ONES3[:, 64:192] if r == 0 else ONES3[:, 0:128]
                    rhs = PT[u][:, r, uoff:uoff + qhi - qlo]
                    fns.append(lambda lhs_v=lhs_v, rhs=rhs, qlo=qlo, qhi=qhi, st=st, sp_=sp_: mm(
                        numb[:, qlo:qhi], lhs_v, rhs, start=st, stop=sp_))
                    fns.append(lambda lhs_1=lhs_1, rhs=rhs, qlo=qlo, qhi=qhi, st=st, sp_=sp_: mm(
                        denb[:, qlo:qhi], lhs_1, rhs, start=st, stop=sp_))
                    k += 1
            sc.group(pe, fns, [r_PT[u]] + [r_V[(2 * b - 4 + t) % 6] for t in tiles], [numr, denr])
            if last:
                tg, tgr = tgs[j]
                rd_, rdr = t1next()
                sc.op(dve, lambda: nc.vector.reciprocal(out=rd_[:], in_=denb[:, 0:256]), [denr], [rdr])
                sc.op(dve, lambda: nc.vector.tensor_tensor(out=rd_[:], in0=numb[:, 0:256], in1=rd_[:], op=ALU.mult),
                      [numr, rdr], [rdr])
                sc.op(pool, lambda: nc.gpsimd.tensor_tensor(out=ya[b % 2][:, j, :], in0=rd_[:], in1=tg[:], op=ALU.mult),
                      [rdr, tgr], [r_ya[b % 2][j]])
                gunpin(numr)
                gunpin(denr)

        assert units[0] == 2
        pipelined = len(units) >= 2
        prev = None
        for (j, u) in items:
            if u == units[0]:
                gate(j)
            scores(j, u)
            yield 0
            if not pipelined:
                pv(j, u)
                yield 3
                continue
            if prev is not None:
                pv(*prev)
                yield (3 if prev[1] == units[-1] else 1)
            prev = (j, u)
        if prev is not None:
            pv(*prev)
            yield 3

    def sgu(b):
        for g in range(4):
            gb, gr = gnext(pin=True)
            gv = gb[:].rearrange("p (a b) -> p a b", a=2)
            sc.group(pe, proj_fm(b, gv[:, 0, :], C_UB + g * 128) + proj_fm(b, gv[:, 1, :], C_GB + g * 128),
                     r_hT[b % 2] + wr_in(C_UB + g * 128) + wr_in(C_GB + g * 128), [gr])
            tu, tur = t1next()
            tb, tbr = t1next()
            sc.op(act, lambda: nc.scalar.activation(out=tu[:], in_=gv[:, 0, :], func=AF.Square, scale=GELU_A ** 0.5),
                  [gr], [tur])
            sc.op(act, lambda: nc.scalar.activation(out=tb[:], in_=gv[:, 1, :], func=AF.Tanh, scale=0.5), [gr], [tbr])
            yield
            sc.op(dve, lambda: nc.vector.scalar_tensor_tensor(out=tu[:], in0=tu[:], scalar=1.0, in1=gv[:, 0, :],
                                                              op0=ALU.add, op1=ALU.mult), [gr, tur], [tur])
            sc.op(act, lambda: nc.scalar.activation(out=tu[:], in_=tu[:], func=AF.Tanh, scale=GELU_C), [tur], [tur])
            yield
            sc.op(dve, lambda: nc.vector.scalar_tensor_tensor(out=tu[:], in0=tu[:], scalar=1.0, in1=gv[:, 0, :],
                                                              op0=ALU.add, op1=ALU.mult), [gr, tur], [tur])
            sc.op(dve, lambda: nc.vector.scalar_tensor_tensor(out=tb[:], in0=tb[:], scalar=1.0, in1=gv[:, 1, :],
                                                              op0=ALU.add, op1=ALU.mult), [gr, tbr], [tbr])
            gunpin(gr)
            sc.op(pool, lambda: nc.gpsimd.tensor_tensor(out=tu[:], in0=tu[:], in1=tb[:], op=ALU.mult), [tur, tbr], [tur])
            yield
            mb, mr = gnext()
            sc.group(pe, [(lambda i=i: mm(mb[:, i * 128:(i + 1) * 128], vn[i][:, g * 128:(g + 1) * 128], WT[:, g, :],
                                          start=True, stop=True)) for i in range(2)], r_vn, [mr])
            sc.op(dve, lambda: nc.vector.scalar_tensor_tensor(
                out=tb[:].rearrange("p (a b) -> p a b", a=2), in0=mb[:, 0:256].rearrange("p (a b) -> p a b", a=2),
                scalar=lng[:, g:g + 1], in1=bcast_mid(Cg[:, g, :], 2), op0=ALU.mult, op1=ALU.add),
                [mr, tbr, r_setup], [tbr])
            sc.op(dve, lambda: nc.vector.tensor_tensor(out=yb[b % 2][:, g, :], in0=tb[:], in1=tu[:], op=ALU.mult),
                  [tbr, tur], [r_yb[b % 2][g]])
            yield

    def phaseD(b):
        for j in range(8):
            gb, gr = gnext()
            gv = gb[:].rearrange("p (a b) -> p a b", a=2)
            sc.group(pe, proj_fm(b, gv[:, 0, :], C_GTA + j * 128) + proj_fm(b, gv[:, 1, :], C_GTB + j * 128),
                     r_hT[b % 2] + wr_in(C_GTA + j * 128) + wr_in(C_GTB + j * 128), [gr])
            sc.op(act, lambda: nc.scalar.activation(out=Gt[:, 0, :], in_=gv[:, 0, :], func=AF.Tanh, bias=hbg[:, j:j + 1],
                                                    scale=0.5), [gr, r_c["hbg"]], [r_Gt])
            sc.op(act, lambda: nc.scalar.activation(out=Gt[:, 1, :], in_=gv[:, 1, :], func=AF.Tanh,
                                                    bias=hbg[:, 8 + j:9 + j], scale=0.5), [gr, r_c["hbg"]], [r_Gt])
            yield
            pb_, pr = gnext()
            pv = pb_[:].rearrange("p (a b) -> p a b", a=2)
            fns = [(lambda ec=ec: mm(pv[:, 0, :], Wpa[:, ec, j * 128:(j + 1) * 128], ya[b % 2][:, ec, :], start=(ec == 0),
                                     stop=(ec == 3))) for ec in range(4)]
            fns += [(lambda ec=ec: mm(pv[:, 1, :], Wpb[:, ec, j * 128:(j + 1) * 128], yb[b % 2][:, ec, :], start=(ec == 0),
                                      stop=(ec == 3))) for ec in range(4)]
            sc.group(pe, fns, r_ya[b % 2] + r_yb[b % 2] + [Wres[("pa", 0, ec)] for ec in range(4)] + [Wres[("pb", 0, ec)] for ec in range(4)],
                     [pr])
            sc.op(dve, lambda: nc.vector.scalar_tensor_tensor(out=M1[:], in0=Gt[:], scalar=1.0, in1=pv[:, :, :],
                                                              op0=ALU.add, op1=ALU.mult), [pr, r_Gt], [r_M1])
            sc.op(pool, lambda: nc.gpsimd.tensor_tensor(out=merged[:, j, :], in0=M1[:, 0, :], in1=M1[:, 1, :], op=ALU.add),
                  [r_M1], [r_m[j]])
            yield

    def phaseE(b):
        slots = []
        for i in range(2):
            ti = 2 * b + i
            ft, fr, fd = fnext()
            sc.dma(sp, fd, ft[:], x_d[ti * 128:(ti + 1) * 128, :], [], [fr])
            slots.append((ft, fr, fd))
            for hf in range(2):
                gb, gr = gnext()
                sc.group(pe, [(lambda dc=dc, gb=gb: mm(gb[:, :], merged[:, dc, i * 128:(i + 1) * 128],
                                                        Wout[:, dc, hf * 512:(hf + 1) * 512], start=(dc == 0), stop=(dc == 7)))
                              for dc in range(8)], r_m + [Wres[("out", 0, dc)] for dc in range(8)], [gr])
                sc.op(dve, lambda gb=gb, ft=ft, hf=hf: nc.vector.tensor_tensor(
                    out=ft[:, hf * 512:(hf + 1) * 512], in0=gb[:, :], in1=ft[:, hf * 512:(hf + 1) * 512], op=ALU.add),
                    [gr, fr], [fr])
                yield
            sc.op(act, lambda ft=ft, i=i: nc.scalar.activation(out=M1[:].rearrange("p a b -> p (a b)").bitcast(BF16), in_=ft[:], func=AF.Square,
                                                              accum_out=ss2[:, i:i + 1]), [fr], [r_M1, r_ss2])
        sc.op(act, lambda: nc.scalar.activation(out=sb2[:], in_=ss2[:], func=AF.Sqrt, scale=1.0 / D, bias=epsc[:, 0:1]),
              [r_ss2, r_c["epsc"]], [r_sb2])
        yield
        sc.op(dve, lambda: nc.vector.reciprocal(out=rstd2[:], in_=sb2[:]), [r_sb2], [r_rstd2])
        for i in range(2):
            ti = 2 * b + i
            ft, fr, fd = slots[i]
            sc.op(dve,
                  lambda ft=ft, i=i: nc.vector.scalar_tensor_tensor(
                      out=ft[:], in0=ft[:], scalar=rstd2[:, i:i + 1], in1=fgb[:], op0=ALU.mult, op1=ALU.mult),
                  [fr, r_rstd2, r_setup], [fr])
            sc.dma(sp, fd, y_d[ti * 128:(ti + 1) * 128, :], ft[:], [fr], [])
            fr.rs[fd.sem.name] = (fd.sem, 16 * fd.cnt)
            yield

    def drain(g):
        for _ in g:
            pass

    def chain(*gens):
        for g in gens:
            yield from g

    def interleave(ga, gb_):
        for hint in ga:
            for _ in range(1 if hint is None else hint):
                next(gb_, None)
        drain(gb_)

    def wgen(jobs):
        for job in jobs:
            emit_wjob(job)
            yield

    set_wpool("all")
    for job in wjobs[:24]:
        emit_wjob(job)
    drain(normA(0))
    drain(trans(0))
    if nblocks > 1:
        drain(normA(1))
    drain(projB(0))
    set_wpool("late")
    fence(XAres)
    interleave(chain(attention(0), sgu(0)), wgen(wjobs[24:]))
    fence(XBres)
    def stage12(b):
        yield from trans(b)
        yield from projB(b, vb=(), qk=(0,), v=(0, 1))
        pending = [projB(b, vb=(), qk=(1,), v=()), projB(b, vb=(), qk=(2,), v=()), projB(b, vb=(), qk=(3,), v=()),
                   projB(b, vb=(0,), qk=(), v=()), projB(b, vb=(1,), qk=(), v=())]
        k = 0
        for hint in attention(b):
            yield hint
            k += 1
            if k % 2 == 0 and pending:
                yield from pending.pop(0)
        for g in pending:
            yield from g
        yield from sgu(b)

    for b in range(nblocks):
        if b + 1 < nblocks:
            parts = []
            if b >= 1:
                parts.append(phaseE(b - 1))
            parts.append(phaseD(b))
            if b + 2 < nblocks:
                parts.append(normA(b + 2))
            interleave(stage12(b + 1), chain(*parts))
        else:
            if b >= 1:
                drain(phaseE(b - 1))
            drain(chain(phaseD(b), phaseE(b)))
    for fd in Fd:
        sc._wait(sp, fd.sem, 16 * fd.cnt)
    nc._sched_stats = (sc.nwait, {e.name: e.cnt for e in sc.engs})
    return nc


_CONST = {}


def _consts():
    if not _CONST:
        _CONST["ident"] = np.eye(128, dtype=np.float32)
        s = np.arange(128)[:, None]
        t = np.arange(128)[None, :]
        _CONST["trilT"] = (s <= t).astype(np.float32)
    return _CONST


def kernel(x, norm_g, w_in, b_gate, rel_bias, sgu_ln_g, sgu_ln_b, w_s, b_s, w_pa, w_pb, w_out, final_g):
    nblocks = NB
    f = lambda a: np.ascontiguousarray(np.asarray(a, dtype=np.float32))
    x = f(x)
    c = _consts()
    shared = {
        "norm_g": f(norm_g)[0], "w_in": f(w_in)[0], "b_gate": f(b_gate)[0], "rel_bias": f(rel_bias)[0],
        "sgu_ln_g": f(sgu_ln_g)[0], "sgu_ln_b": f(sgu_ln_b)[0], "w_s": f(w_s)[0], "b_s": f(b_s)[0].reshape(512),
        "w_pa": f(w_pa)[0], "w_pb": f(w_pb)[0], "w_out": f(w_out)[0], "final_g": f(final_g),
        "ident": c["ident"], "trilT": c["trilT"],
    }
    nc = build_nc(nblocks)
    in_maps = [dict(shared, x=x[i]) for i in range(8)]
    res = run_bass_kernel_spmd(nc, in_maps, core_ids=list(range(8)))
    return np.stack([np.asarray(r["y"], dtype=np.float32) for r in res.results], axis=0)
```

```python
import numpy as np
import concourse.bass as bass
import concourse.mybir as mybir
from concourse.bass_utils import run_bass_kernel_spmd

F32 = mybir.dt.float32
BF16 = mybir.dt.bfloat16
ALU = mybir.AluOpType
AF = mybir.ActivationFunctionType

D = 1024
S = 4096
DIN = 5632
NB = 16
BT = 256
EPS = 1e-6
C_Q, C_K, C_V, C_GA, C_UB, C_VB, C_GB, C_GTA, C_GTB = 0, 512, 1024, 1536, 2048, 2560, 3072, 3584, 4608
GELU_C = 0.7978845608028654
GELU_A = 0.044715


class Res:
    __slots__ = ("name", "w", "rs", "psum", "touch")

    def __init__(self, name, psum=False):
        self.name = name
        self.w = None
        self.rs = {}
        self.psum = psum
        self.touch = 0


class DmaSem:
    def __init__(self, nc, name):
        self.sem = nc.alloc_semaphore(name)
        self.cnt = 0


class Eng:
    def __init__(self, nc, h, name):
        self.h = h
        self.name = name
        self.sem = nc.alloc_semaphore("s_" + name)
        self.cnt = 0
        self.waited = {}


class Sched:
    def __init__(self, nc):
        self.nc = nc
        self.pe = Eng(nc, nc.tensor, "pe")
        self.act = Eng(nc, nc.scalar, "act")
        self.dve = Eng(nc, nc.vector, "dve")
        self.pool = Eng(nc, nc.gpsimd, "pool")
        self.sp = Eng(nc, nc.sync, "sp")
        self.engs = [self.pe, self.act, self.dve, self.pool, self.sp]
        self.nwait = 0
        self.clock = {}

    def _wait(self, e, sem, val):
        if e.waited.get(sem.name, 0) >= val:
            return
        e.h.wait_ge(sem, val)
        e.waited[sem.name] = val
        self.nwait += 1
        for k, v in self.clock.get((sem.name, val), {}).items():
            if e.waited.get(k, 0) < v:
                e.waited[k] = v

    def _snap(self, e, ev):
        c = dict(e.waited)
        if e is not self.sp:
            c[e.sem.name] = e.cnt
        self.clock[(ev[0].name, ev[1])] = c

    def deps(self, e, reads, writes):
        for r in reads:
            if r.w is not None:
                sem, val = r.w
                if not (sem is e.sem and e is self.pe):
                    self._wait(e, sem, val)
            if r.psum:
                for sem, val in r.rs.values():
                    if sem is not e.sem:
                        self._wait(e, sem, val)
        for w in writes:
            if w.w is not None:
                sem, val = w.w
                if not (sem is e.sem and e is self.pe):
                    self._wait(e, sem, val)
            for sem, val in w.rs.values():
                if not (sem is e.sem and e is self.pe):
                    self._wait(e, sem, val)

    def _record(self, ev, reads, writes):
        self.tick = getattr(self, "tick", 0) + 1
        for r in reads:
            r.rs[ev[0].name] = ev
            r.touch = self.tick
        for w in writes:
            w.w = ev
            w.rs = {}
            w.touch = self.tick

    def op(self, e, fn, reads=(), writes=()):
        self.deps(e, reads, writes)
        ins = fn()
        e.cnt += 1
        ins.then_inc(e.sem, 1)
        self._snap(e, (e.sem, e.cnt))
        self._record((e.sem, e.cnt), reads, writes)

    def group(self, e, fns, reads=(), writes=()):
        self.deps(e, reads, writes)
        ins = None
        for fn in fns:
            ins = fn()
        e.cnt += 1
        ins.then_inc(e.sem, 1)
        self._snap(e, (e.sem, e.cnt))
        self._record((e.sem, e.cnt), reads, writes)

    def dma(self, e, ds, out, in_, reads=(), writes=(), **kw):
        self.deps(e, reads, writes)
        ins = e.h.dma_start(out=out, in_=in_, **kw)
        ds.cnt += 1
        ins.then_inc(ds.sem, 16)
        self._snap(e, (ds.sem, 16 * ds.cnt))
        self._record((ds.sem, 16 * ds.cnt), reads, writes)

    def barrier(self):
        for e in self.engs:
            for o in self.engs:
                if o is not e and o.cnt > 0:
                    self._wait(e, o.sem, o.cnt)


def build_nc(nblocks=NB, dbg=False):
    nc = bass.Bass("TRN2", target_bir_lowering=False)
    dt = nc.dram_tensor
    x_d = dt("x", [S, D], F32, kind="ExternalInput").ap()
    ng_d = dt("norm_g", [D], F32, kind="ExternalInput").ap()
    win_d = dt("w_in", [D, DIN], F32, kind="ExternalInput").ap()
    bg_d = dt("b_gate", [2 * D], F32, kind="ExternalInput").ap()
    rb_d = dt("rel_bias", [8, 257], F32, kind="ExternalInput").ap()
    lng_d = dt("sgu_ln_g", [512], F32, kind="ExternalInput").ap()
    lnb_d = dt("sgu_ln_b", [512], F32, kind="ExternalInput").ap()
    ws_d = dt("w_s", [4, 128, 128], F32, kind="ExternalInput").ap()
    bs_d = dt("b_s", [512], F32, kind="ExternalInput").ap()
    wpa_d = dt("w_pa", [512, D], F32, kind="ExternalInput").ap()
    wpb_d = dt("w_pb", [512, D], F32, kind="ExternalInput").ap()
    wout_d = dt("w_out", [D, D], F32, kind="ExternalInput").ap()
    fg_d = dt("final_g", [D], F32, kind="ExternalInput").ap()
    id_d = dt("ident", [128, 128], F32, kind="ExternalInput").ap()
    tr_d = dt("trilT", [128, 128], F32, kind="ExternalInput").ap()
    y_d = dt("y", [S, D], F32, kind="ExternalOutput").ap()
    ext_d = dt("ext_scr", [8, 384], F32).ap()
    t2_d = dt("toe_scr", [8, 128, 256], F32).ap()

    def sb(name, shape, dtype):
        return nc.alloc_sbuf_tensor(name, shape, dtype)

    def dsz(dtype):
        return 4 if dtype == F32 else 2

    W = sb("W", [128, 8, DIN], BF16)
    Wpa = sb("Wpa", [128, 4, D], BF16)
    Wpb = sb("Wpb", [128, 4, D], BF16)
    Wout = sb("Wout", [128, 8, D], BF16)
    NF = 3
    Fs = [sb(f"F{i}", [128, D], F32) for i in range(NF)]
    hb = [sb(f"hb{i}", [128, D], BF16) for i in range(2)]
    hT = [sb(f"hT{i}", [128, 8, BT], BF16) for i in range(2)]
    qT = sb("qT", [128, 4, BT], BF16)
    kT = sb("kT", [128, 4, 768], BF16)
    Vr = sb("Vr", [128, 6, 768], BF16)
    PT = [sb("PT0", [128, 2, 512], BF16)]
    pt0 = nc.sbuf_base - 2048
    PT += [sb(f"PT{i}", [128, 2, 512], BF16) for i in (1, 2)]
    NT1 = 4
    T1 = [sb(f"T1_{i}", [128, BT], F32) for i in range(NT1)]
    vf = sb("vf", [128, 512], F32)
    vsq = sb("vsq", [128, 512], F32)
    vn = [sb(f"vn{i}", [128, 512], BF16) for i in range(2)]
    reg0 = nc.sbuf_base
    merged = sb("merged", [128, 8, BT], BF16)
    reg0 = nc.sbuf_base - 8 * BT * 2
    Gt = sb("Gt", [128, 2, BT], F32)
    M1 = sb("M1", [128, 2, BT], F32)
    ya0 = sb("ya0", [128, 4, BT], BF16)
    yb0 = sb("yb0", [128, 4, BT], BF16)
    assert nc.sbuf_base - reg0 == 12288, (nc.sbuf_base, reg0)
    _off = [reg0]

    def sbat(name, shape, dtype):
        n = int(np.prod(shape[1:])) * dsz(dtype)
        t = nc.alloc_sbuf_tensor_at(name, shape, dtype, offset=_off[0])
        _off[0] += (n + 31) // 32 * 32
        assert _off[0] <= reg0 + 12288
        return t

    ident_f = sbat("ident_f", [128, 128], F32)
    prow = sbat("prow", [28, 128], F32)
    tril_f = sbat("tril_f", [128, 128], F32)
    ws_f = sbat("ws_f", [128, 4, 128], F32)
    ws_b = sbat("ws_b", [128, 4, 128], BF16)
    bsb = sbat("bsb", [128, 512], F32)
    lnb_bc = sbat("lnb_bc", [128, 512], F32)
    lnb_hi = sbat("lnb_hi", [128, 512], BF16)
    lnb_lo = sbat("lnb_lo", [128, 512], BF16)
    ya = [ya0, sb("ya1", [128, 4, BT], BF16)]
    yb = [yb0, sb("yb1", [128, 4, BT], BF16)]
    XA = [nc.alloc_sbuf_tensor_at(f"XA{i}", [128, D], F32, offset=pt0 + 4096 * i) for i in range(2)]
    XB = [nc.alloc_sbuf_tensor_at(f"XB{i}", [128, D], F32, offset=reg0 + 4096 * i) for i in range(2)]
    fgb = sb("fgb", [128, D], F32)
    Cg = sb("Cg", [128, 4, 128], F32)
    EBX = sb("EBX", [128, 8, 256], BF16)
    WT = sb("WT", [128, 4, 128], BF16)
    identb = sb("identb", [128, 128], BF16)
    ONES3 = sb("ONES3", [128, 192], BF16)
    ng = sb("ng", [128, 8], F32)
    bg = sb("bg", [128, 16], F32)
    hbg = sb("hbg", [128, 16], F32)
    lng = sb("lng", [128, 4], F32)
    ss = sb("ss", [128, 2], F32)
    sa2 = sb("sa2", [128, 2], F32)
    rstd = sb("rstd", [128, 2], F32)
    ss2 = sb("ss2", [128, 2], F32)
    sb2 = sb("sb2", [128, 2], F32)
    rstd2 = sb("rstd2", [128, 2], F32)
    s1 = sb("s1", [128, 2], F32)
    s2 = sb("s2", [128, 2], F32)
    mu = sb("mu", [128, 2], F32)
    msq = sb("msq", [128, 2], F32)
    var = sb("var", [128, 2], F32)
    rstdv = sb("rstdv", [128, 2], F32)
    rbs = sb("rbs", [8, 257], F32)
    es = sb("es", [8, 384], F32)
    nbias = sb("nbias", [8, 1], F32)
    epsc = sb("epsc", [128, 2], F32)

    NG = 8
    Db = [nc.alloc_psum_tensor(f"Db{i}", [128, 2, 512], F32) for i in range(4)]
    Gb = [Db[i // 2][:, i % 2, :] for i in range(NG)]

    sc = Sched(nc)
    pe, act, dve, pool, sp = sc.pe, sc.act, sc.dve, sc.pool, sc.sp
    R = Res
    Gres = [R(f"G{i}", True) for i in range(NG)]
    Fres = [R(f"F{i}") for i in range(NF)]
    Fd = [DmaSem(nc, f"d_F{i}") for i in range(NF)]
    XAres = [R(f"XA{i}") for i in range(2)]
    XBres = [R(f"XB{i}") for i in range(2)]
    XAd = [DmaSem(nc, f"d_XA{i}") for i in range(2)]
    XBd = [DmaSem(nc, f"d_XB{i}") for i in range(2)]
    setup_d = DmaSem(nc, "d_setup")
    chain_d = DmaSem(nc, "d_chain")
    cnt = {"g": 0, "f": 0, "t1": 0, "s": 0, "nd": 0, "cast": 0}

    pinned = set()

    def gnext(pin=False):
        free = [i for i in range(NG) if i not in pinned]
        assert free, "all PSUM banks pinned"
        i = min(free, key=lambda k: Gres[k].touch)
        if pin:
            pinned.add(i)
        Gres[i].touch = getattr(sc, "tick", 0) + 1
        return Gb[i], Gres[i]

    def gunpin(res):
        pinned.discard(Gres.index(res))

    def snext():
        free = [i for i in range(0, NG, 2) if i not in pinned and (i + 1) not in pinned]
        assert free, "no free PSUM bank pair"
        i = min(free, key=lambda k: max(Gres[k].touch, Gres[k + 1].touch))
        Gres[i].touch = Gres[i + 1].touch = getattr(sc, "tick", 0) + 1
        return Db[i // 2], [Gres[i], Gres[i + 1]]

    def fnext():
        i = cnt["f"] % NF
        cnt["f"] += 1
        return Fs[i], Fres[i], Fd[i]

    T1res = [R(f"T1_{i}") for i in range(NT1)]

    def t1next():
        i = cnt["t1"] % NT1
        cnt["t1"] += 1
        return T1[i], T1res[i]

    mm = nc.tensor.matmul

    def bcast_mid(a, n):
        pat = [list(p) for p in a.ap]
        return bass.AP(a.tensor, a.offset, [pat[0], [0, n]] + pat[1:])

    r_setup = R("setup_in")

    def sdma(out, in_, **kw):
        sc.dma(sp, setup_d, out, in_, reads=(), writes=(r_setup,), **kw)

    sdma(ident_f[:], id_d[:, :])
    sdma(tril_f[:], tr_d[:, :])
    sdma(ws_f[:], ws_d.rearrange("g t s -> t g s"))
    sdma(bsb[:], bs_d.partition_broadcast(128))
    sdma(lnb_bc[:], lnb_d.partition_broadcast(128))
    sdma(fgb[:], fg_d.partition_broadcast(128))
    sdma(prow[0:8, :], ng_d.rearrange("(r p) -> r p", p=128))
    sdma(prow[8:24, :], bg_d.rearrange("(r p) -> r p", p=128))
    sdma(prow[24:28, :], lng_d.rearrange("(r p) -> r p", p=128))
    sdma(rbs[:], rb_d[:, :])
    r_setup.w = (setup_d.sem, 16 * setup_d.cnt)

    r_c = {k: R(k) for k in ["identb", "hbg", "ws_b", "WT", "lnb_hi", "lnb_lo", "Cg", "nb", "es", "ext", "t2",
                             "EBX", "ONES3", "Vr0", "epsc"]}
    sc.op(pool, lambda: nc.gpsimd.memset(epsc[:, 0:1], EPS), [], [r_c["epsc"]])
    sc.op(pool, lambda: nc.gpsimd.memset(epsc[:, 1:2], 4.0 * EPS), [r_c["epsc"]], [r_c["epsc"]])
    sc.op(dve, lambda: nc.vector.tensor_copy(out=identb[:], in_=ident_f[:]), [r_setup], [r_c["identb"]])
    gb, gr = gnext()
    sc.group(pe, [lambda: mm(gb[:, 0:28], prow[0:28, :], ident_f[0:28, 0:28], start=True, stop=True)], [r_setup], [gr])
    r_c["pcols"] = R("pcols")
    sc.op(dve, lambda: nc.vector.tensor_copy(out=ng[:], in_=gb[:, 0:8]), [gr], [r_c["pcols"]])
    sc.op(dve, lambda: nc.vector.tensor_copy(out=lng[:], in_=gb[:, 24:28]), [gr], [r_c["pcols"]])
    sc.op(dve, lambda: nc.vector.tensor_scalar(out=hbg[:], in0=gb[:, 8:24], scalar1=0.5, scalar2=None, op0=ALU.mult),
          [gr], [r_c["hbg"]])
    sc.op(act, lambda: nc.scalar.copy(out=ws_b[:], in_=ws_f[:]), [r_setup], [r_c["ws_b"]])
    gb, gr = gnext()
    gbv = gb[:].bitcast(BF16).rearrange("p (a b) -> p a b", a=8)
    sc.group(pe, [(lambda g=g: nc.tensor.transpose(out=gbv[:, g, :], in_=ws_b[:, g, :], identity=identb[:]))
                  for g in range(4)], [r_c["ws_b"], r_c["identb"]], [gr])
    sc.op(dve, lambda: nc.vector.tensor_tensor(out=WT[:], in0=gbv[:, 0:4, :], in1=bcast_mid(tril_f[:], 4), op=ALU.mult),
          [gr, r_setup], [r_c["WT"]])
    sc.op(dve, lambda: nc.vector.tensor_copy(out=lnb_hi[:], in_=lnb_bc[:]), [r_setup], [r_c["lnb_hi"]])
    sc.op(dve, lambda: nc.vector.tensor_tensor(out=lnb_lo[:], in0=lnb_bc[:], in1=lnb_hi[:], op=ALU.subtract),
          [r_setup, r_c["lnb_hi"]], [r_c["lnb_lo"]])
    gb, gr = gnext()
    gcv = gb[:].rearrange("p (a b) -> p a b", a=4)
    fns = []
    for g in range(4):
        fns.append(lambda g=g: mm(gcv[:, g, :], lnb_hi[:, g * 128:(g + 1) * 128], WT[:, g, :], start=True, stop=False))
        fns.append(lambda g=g: mm(gcv[:, g, :], lnb_lo[:, g * 128:(g + 1) * 128], WT[:, g, :], start=False, stop=True))
    sc.group(pe, fns, [r_c["lnb_hi"], r_c["lnb_lo"], r_c["WT"]], [gr])
    sc.op(dve, lambda: nc.vector.tensor_tensor(out=Cg[:], in0=gcv[:, :, :],
                                               in1=bsb[:].rearrange("p (a b) -> p a b", a=4), op=ALU.add),
          [gr, r_setup], [r_c["Cg"]])
    sc.op(dve, lambda: nc.vector.tensor_scalar(out=nbias[:], in0=rbs[:, 256:257], scalar1=-1.0, scalar2=None,
                                               op0=ALU.mult), [r_setup], [r_c["nb"]])
    sc.op(pool, lambda: nc.gpsimd.memset(es[:], 1.0), [], [r_c["es"]])
    sc.op(act, lambda: nc.scalar.activation(out=es[:, 0:257], in_=rbs[:], func=AF.Exp, bias=nbias[:, 0:1], scale=1.0),
          [r_setup, r_c["nb"], r_c["es"]], [r_c["es"]])
    sc.dma(sp, chain_d, ext_d[:, :], es[:], [r_c["es"]], [r_c["ext"]])
    sc.dma(sp, chain_d, t2_d[:, :, :], bass.AP(ext_d.tensor, 128, [[384, 8], [-1, 128], [1, 256]]),
           [r_c["ext"]], [r_c["t2"]])
    for half in range(2):
        ft, fr, fd = fnext()
        sc.dma(sp, fd, ft[:].rearrange("p (a b) -> p a b", a=4),
               t2_d[4 * half:4 * half + 4, :, :].rearrange("h k j -> k h j"), [r_c["t2"]], [fr])
        sc.op(dve, lambda ft=ft, half=half: nc.vector.tensor_copy(
            out=EBX[:, 4 * half:4 * half + 4, 0:256], in_=ft[:].rearrange("p (a b) -> p a b", a=4)),
            [fr], [r_c["EBX"]])
    sc.op(pool, lambda: nc.gpsimd.memset(EBX[64:128, :, 0:64], 0.0), [r_c["EBX"]], [r_c["EBX"]])
    sc.op(pool, lambda: nc.gpsimd.memset(ONES3[:], 0.0), [], [r_c["ONES3"]])
    sc.op(pool, lambda: nc.gpsimd.memset(ONES3[:, 64:128], 1.0), [r_c["ONES3"]], [r_c["ONES3"]])
    sc.op(pool, lambda: nc.gpsimd.memset(Vr[:], 0.0), [], [r_c["Vr0"]])
    sc.barrier()

    Wres = {}
    wjobs = []
    for p in range(6):
        c0 = 1024 * p
        c1 = min(c0 + 1024, DIN)
        for kc in range(8):
            wjobs.append(("in", p, kc, win_d[kc * 128:(kc + 1) * 128, c0:c1], W[:, kc, c0:c1], c1 - c0, ng[:, kc:kc + 1]))
    for kc in range(4):
        wjobs.append(("pa", 0, kc, wpa_d[kc * 128:(kc + 1) * 128, :], Wpa[:, kc, :], D, 0.5))
    for kc in range(4):
        wjobs.append(("pb", 0, kc, wpb_d[kc * 128:(kc + 1) * 128, :], Wpb[:, kc, :], D, 0.25))
    for kc in range(8):
        wjobs.append(("out", 0, kc, wout_d[kc * 128:(kc + 1) * 128, :], Wout[:, kc, :], D, 0.5))

    wpool = {"slots": None, "i": 0}

    def set_wpool(kind):
        base = [(Fs[i], Fres[i], Fd[i]) for i in range(NF)]
        xa = [(XA[i], XAres[i], XAd[i]) for i in range(2)]
        xb = [(XB[i], XBres[i], XBd[i]) for i in range(2)]
        wpool["slots"] = {"all": base + xa + xb, "late": base + xb, "base": base}[kind]

    def fence(ress):
        for e in (pe, act, dve, pool):
            sc.deps(e, [], ress)

    def emit_wjob(job):
        name, p, kc, src, dst, n, scal = job
        ft, fr, fd = wpool["slots"][wpool["i"] % len(wpool["slots"])]
        wpool["i"] += 1
        sc.dma(sp, fd, ft[:, 0:n], src, [], [fr])
        res = R(f"W{name}{p}_{kc}")
        Wres[(name, p, kc)] = res
        k = cnt["cast"]
        cnt["cast"] += 1
        if k % 2 == 0:
            sc.op(act, lambda: nc.scalar.mul(out=dst, in_=ft[:, 0:n], mul=scal), [fr, r_setup], [res])
        else:
            sc.op(dve, lambda: nc.vector.tensor_scalar(out=dst, in0=ft[:, 0:n], scalar1=scal, scalar2=None, op0=ALU.mult),
                  [fr, r_setup], [res])

    def wr_in(c0, n=128):
        p0, p1 = c0 // 1024, (c0 + n - 1) // 1024
        return [Wres[("in", p, kc)] for p in range(p0, p1 + 1) for kc in range(8)]

    r_hb = [R("hb0"), R("hb1")]
    r_junk = R("junk")
    r_ss, r_sa2, r_rstd = R("ss"), R("sa2"), R("rstd")
    r_hT = [[R(f"hT{p}_{i}") for i in range(2)] for p in range(2)]
    r_q = [R(f"q{j}") for j in range(4)]
    r_k = [[R(f"k{s}_{j}") for j in range(4)] for s in range(3)]
    r_V = [R(f"V{s}") for s in range(6)]
    r_PT = [R(f"PT{u}") for u in range(3)]
    r_vf, r_vsq = R("vf"), R("vsq")
    r_vn = [R("vn0"), R("vn1")]
    r_s1, r_s2, r_mu, r_msq, r_var, r_rstdv = R("s1"), R("s2"), R("mu"), R("msq"), R("var"), R("rstdv")
    r_ya = [[R(f"ya{p}_{j}") for j in range(4)] for p in range(2)]
    r_yb = [[R(f"yb{p}_{g}") for g in range(4)] for p in range(2)]
    r_Gt, r_M1 = R("Gt"), R("M1")
    r_m = [R(f"m{j}") for j in range(8)]
    r_ss2, r_sb2, r_rstd2 = R("ss2"), R("sb2"), R("rstd2")
    xin = {}

    def normA(b):
        for i in range(2):
            ti = 2 * b + i
            ft, fr, fd = fnext()
            sc.dma(sp, fd, ft[:], x_d[ti * 128:(ti + 1) * 128, :], [], [fr])
            xin[ti] = (ft, fr)
            sc.op(act, lambda ft=ft, i=i: nc.scalar.activation(out=hb[i][:], in_=ft[:], func=AF.Square,
                                                              accum_out=ss[:, i:i + 1]),
                  [fr], [r_hb[i], r_ss])
        yield
        sc.op(act, lambda: nc.scalar.activation(out=sa2[:], in_=ss[:], func=AF.Sqrt, scale=1.0 / D, bias=epsc[:, 0:1]),
              [r_ss, r_c["epsc"]], [r_sa2])
        yield
        sc.op(dve, lambda: nc.vector.reciprocal(out=rstd[:], in_=sa2[:]), [r_sa2], [r_rstd])
        yield
        for i in range(2):
            ft, fr = xin[2 * b + i]
            sc.op(act, lambda ft=ft, i=i: nc.scalar.mul(out=hb[i][:], in_=ft[:], mul=rstd[:, i:i + 1]),
                  [fr, r_rstd], [r_hb[i]])
        yield

    def trans(b):
        for i in range(2):
            for half in range(2):
                gb, gr = gnext()
                gv = gb[:].rearrange("p (a b) -> p a b", a=4)
                sc.group(pe, [(lambda c=c, gv=gv: mm(gv[:, c, :], hb[i][:, (4 * half + c) * 128:(4 * half + c + 1) * 128],
                                                     identb[:], start=True, stop=True)) for c in range(4)],
                         [r_hb[i]], [gr])
                if half == 0:
                    sc.op(dve, lambda gv=gv: nc.vector.tensor_copy(out=hT[b % 2][:, 0:4, i * 128:(i + 1) * 128], in_=gv[:, :, :]),
                          [gr], [r_hT[b % 2][i]])
                else:
                    sc.op(act, lambda gv=gv: nc.scalar.copy(out=hT[b % 2][:, 4:8, i * 128:(i + 1) * 128], in_=gv[:, :, :]),
                          [gr], [r_hT[b % 2][i]])
                yield

    def proj_fm(b, out_ap, c0):
        return [(lambda kc=kc: mm(out_ap, W[:, kc, c0:c0 + 128], hT[b % 2][:, kc, :], start=(kc == 0), stop=(kc == 7)))
                for kc in range(8)]

    def projB(b):
        seg = b % 3
        for i in range(2):
            gb, gr = gnext(pin=True)
            sc.group(pe, [(lambda kc=kc, gb=gb, i=i: mm(gb[:, :], hT[b % 2][:, kc, i * 128:(i + 1) * 128], W[:, kc, C_VB:C_VB + 512],
                                                         start=(kc == 0), stop=(kc == 7))) for kc in range(8)],
                     [r_hT[b % 2][i]] + wr_in(C_VB, 512), [gr])
            sc.op(act, lambda gb=gb: nc.scalar.activation(out=vsq[:], in_=gb[:, :], func=AF.Square, scale=GELU_A ** 0.5),
                  [gr], [r_vsq])
            yield
            sc.op(dve, lambda gb=gb: nc.vector.scalar_tensor_tensor(out=vsq[:], in0=vsq[:], scalar=1.0, in1=gb[:, :],
                                                                    op0=ALU.add, op1=ALU.mult), [gr, r_vsq], [r_vsq])
            sc.op(act, lambda: nc.scalar.activation(out=vsq[:], in_=vsq[:], func=AF.Tanh, scale=GELU_C), [r_vsq], [r_vsq])
            yield
            sc.op(dve, lambda gb=gb: nc.vector.scalar_tensor_tensor(out=vf[:], in0=vsq[:], scalar=1.0, in1=gb[:, :],
                                                                    op0=ALU.add, op1=ALU.mult), [gr, r_vsq], [r_vf])
            gunpin(gr)
            sc.op(dve, lambda: nc.vector.tensor_reduce(out=s1[:, 0:1], in_=vf[:], axis=mybir.AxisListType.X, op=ALU.add),
                  [r_vf], [r_s1])
            sc.op(act, lambda: nc.scalar.activation(out=vsq[:], in_=vf[:], func=AF.Square, accum_out=s2[:, 0:1]),
                  [r_vf, r_vsq], [r_vsq, r_s2])
            yield
            sc.op(dve, lambda: nc.vector.tensor_scalar(out=mu[:, 0:1], in0=s1[:, 0:1], scalar1=1.0 / 512, scalar2=None,
                                                       op0=ALU.mult), [r_s1], [r_mu])
            sc.op(dve, lambda: nc.vector.scalar_tensor_tensor(out=msq[:, 0:1], in0=s1[:, 0:1], scalar=-1.0 / (512.0 * 512.0),
                                                              in1=s1[:, 0:1], op0=ALU.mult, op1=ALU.mult), [r_s1], [r_msq])
            sc.op(dve, lambda: nc.vector.scalar_tensor_tensor(out=var[:, 0:1], in0=s2[:, 0:1], scalar=1.0 / 512,
                                                              in1=msq[:, 0:1], op0=ALU.mult, op1=ALU.add),
                  [r_s2, r_msq], [r_var])
            sc.op(act, lambda: nc.scalar.activation(out=var[:, 0:1], in_=var[:, 0:1], func=AF.Sqrt, scale=1.0,
                                                    bias=epsc[:, 1:2]), [r_var, r_c["epsc"]], [r_var])
            yield
            sc.op(dve, lambda: nc.vector.reciprocal(out=rstdv[:, 0:1], in_=var[:, 0:1]), [r_var], [r_rstdv])
            sc.op(dve, lambda i=i: nc.vector.tensor_scalar(out=vn[i][:], in0=vf[:], scalar1=mu[:, 0:1],
                                                           scalar2=rstdv[:, 0:1], op0=ALU.subtract, op1=ALU.mult),
                  [r_vf, r_mu, r_rstdv], [r_vn[i]])
            yield
        for j in range(4):
            gb, gr = gnext()
            gv = gb[:].rearrange("p (a b) -> p a b", a=2)
            sc.group(pe, proj_fm(b, gv[:, 0, :], C_Q + j * 128) + proj_fm(b, gv[:, 1, :], C_K + j * 128),
                     r_hT[b % 2] + wr_in(C_Q + j * 128) + wr_in(C_K + j * 128), [gr])
            sc.op(act, lambda gv=gv, j=j: nc.scalar.copy(out=qT[:, j, :], in_=gv[:, 0, :]), [gr], [r_q[j]])
            sc.op(dve, lambda gv=gv, j=j: nc.vector.tensor_copy(out=kT[:, j, seg * 256:(seg + 1) * 256], in_=gv[:, 1, :]),
                  [gr], [r_k[seg][j]])
            yield
        for i in range(2):
            ti = 2 * b + i
            slot = ti % 6
            gb, gr = gnext()
            sc.group(pe, [(lambda kc=kc, gb=gb, i=i: mm(gb[:, :], hT[b % 2][:, kc, i * 128:(i + 1) * 128], W[:, kc, C_V:C_V + 512],
                                                         start=(kc == 0), stop=(kc == 7))) for kc in range(8)],
                     [r_hT[b % 2][i]] + wr_in(C_V, 512), [gr])
            vdst = Vr[:, slot, :].rearrange("p (j c) -> p j c", j=4)
            sc.op(act, lambda gb=gb, vdst=vdst: nc.scalar.copy(
                out=vdst[:, :, 0:64], in_=gb[:, :].rearrange("p (j c) -> p j c", j=4)[:, :, 0:64]), [gr], [r_V[slot]])
            sc.op(dve, lambda gb=gb, vdst=vdst: nc.vector.tensor_copy(
                out=vdst[:, :, 128:192], in_=gb[:, :].rearrange("p (j c) -> p j c", j=4)[:, :, 64:128]), [gr], [r_V[slot]])
            yield
    TILES = {0: (0, 256, 0, 128), 1: (0, 0, 0, 256), 2: (1, 0, 0, 256), 3: (1, 256, 0, 256),
             4: (2, 0, 0, 256), 5: (2, 256, 128, 256)}
    UNIT_TILES = {0: (1, 0), 1: (2, 3), 2: (4, 5)}
    UNIT_W = {0: 384, 1: 512, 2: 384}

    def attention(b):
        units = [u for u in (2, 1, 0) if 2 * b - 4 + UNIT_TILES[u][0] >= 0 and 2 * b - 4 + UNIT_TILES[u][1] >= 0]
        items = [(j, u) for j in range(4) for u in units]
        tgs = {}
        nd = {}

        def nd_banks(j):
            if j not in nd:
                nb_, nr_ = gnext(pin=True)
                db_, dr_ = gnext(pin=True)
                nd[j] = (nb_, nr_, db_, dr_)
            return nd[j]

        def gate(j):
            gb, gr = gnext()
            sc.group(pe, proj_fm(b, gb[:, 0:256], C_GA + j * 128), r_hT[b % 2] + wr_in(C_GA + j * 128), [gr])
            tg, tgr = t1next()
            sc.op(act, lambda: nc.scalar.activation(out=tg[:], in_=gb[:, 0:256], func=AF.Tanh, scale=0.5), [gr], [tgr])
            sc.op(dve, lambda: nc.vector.scalar_tensor_tensor(out=tg[:], in0=tg[:], scalar=1.0, in1=gb[:, 0:256],
                                                              op0=ALU.add, op1=ALU.mult), [gr, tgr], [tgr])
            tgs[j] = (tg, tgr)

        def scores(j, u):
            sbuf, sres2 = snext()
            fns = []
            rd = [r_q[j]]
            for t in UNIT_TILES[u]:
                _, uoff, qlo, qhi = TILES[t]
                gt = 2 * b - 4 + t
                kpos = (gt * 128) % 768
                rd.append(r_k[kpos // 256][j])
                for r in range(2):
                    fns.append(lambda r=r, uoff=uoff, qlo=qlo, qhi=qhi, kpos=kpos: mm(
                        sbuf[:, r, uoff:uoff + qhi - qlo], kT[64 * r:64 * r + 64, j, kpos:kpos + 128],
                        qT[64 * r:64 * r + 64, j, qlo:qhi], start=True, stop=True))
            sc.group(pe, fns, rd, sres2)
            w = UNIT_W[u]
            sc.op(act, lambda: nc.scalar.activation(out=PT[u][:, :, 0:w], in_=sbuf[:, :, 0:w], func=AF.Exp, scale=0.125),
                  sres2, [r_PT[u]])
            if u == 0:
                sc.op(pool, lambda: nc.gpsimd.memset(PT[0][0:64, :, 320:384], 0.0), [r_PT[0]], [r_PT[0]])
                sc.op(pool, lambda: nc.gpsimd.memset(PT[0][0:64, :, 192:256], 0.0), [r_PT[0]], [r_PT[0]])
            elif u == 1:
                sc.op(dve, lambda: nc.vector.tensor_tensor(out=PT[1][:, :, 256:384], in0=PT[1][:, :, 256:384],
                                                           in1=EBX[:, 2 * j:2 * j + 2, 128:256], op=ALU.mult),
                      [r_PT[1]], [r_PT[1]])
            else:
                sc.op(dve, lambda: nc.vector.tensor_tensor(out=PT[2][:, :, 0:256], in0=PT[2][:, :, 0:256],
                                                           in1=EBX[:, 2 * j:2 * j + 2, 0:256], op=ALU.mult),
                      [r_PT[2]], [r_PT[2]])
                sc.op(pool, lambda: nc.gpsimd.tensor_tensor(out=PT[2][:, :, 256:384], in0=PT[2][:, :, 256:384],
                                                            in1=EBX[:, 2 * j:2 * j + 2, 0:128], op=ALU.mult),
                      [r_PT[2]], [r_PT[2]])

        def pv(j, u):
            numb, numr, denb, denr = nd_banks(j)
            first = (u == units[0])
            last = (u == units[-1])
            tiles = sorted(UNIT_TILES[u])
            fns = []
            n = 2 * len(tiles)
            k = 0
            for t in tiles:
                _, uoff, qlo, qhi = TILES[t]
                slot = (2 * b - 4 + t) % 6
                for r in range(2):
                    st = first and k == 0
                    sp_ = last and k == n - 1
                    lhs_v = Vr[:, slot, 192 * j + 64 * r:192 * j + 64 * r + 128]
                    lhs_1 = ONES3[:, 64:192] if r == 0 else ONES3[:, 0:128]
                    rhs = PT[u][:, r, uoff:uoff + qhi - qlo]
                    fns.append(lambda lhs_v=lhs_v, rhs=rhs, qlo=qlo, qhi=qhi, st=st, sp_=sp_: mm(
                        numb[:, qlo:qhi], lhs_v, rhs, start=st, stop=sp_))
                    fns.append(lambda lhs_1=lhs_1, rhs=rhs, qlo=qlo, qhi=qhi, st=st, sp_=sp_: mm(
                        denb[:, qlo:qhi], lhs_1, rhs, start=st, stop=sp_))
                    k += 1
            sc.group(pe, fns, [r_PT[u]] + [r_V[(2 * b - 4 + t) % 6] for t in tiles], [numr, denr])
            if last:
                tg, tgr = tgs[j]
                rd_, rdr = t1next()
                sc.op(dve, lambda: nc.vector.reciprocal(out=rd_[:], in_=denb[:, 0:256]), [denr], [rdr])
                sc.op(dve, lambda: nc.vector.tensor_tensor(out=rd_[:], in0=numb[:, 0:256], in1=rd_[:], op=ALU.mult),
                      [numr, rdr], [rdr])
                sc.op(pool, lambda: nc.gpsimd.tensor_tensor(out=ya[b % 2][:, j, :], in0=rd_[:], in1=tg[:], op=ALU.mult),
                      [rdr, tgr], [r_ya[b % 2][j]])
                gunpin(numr)
                gunpin(denr)

        assert units[0] == 2
        pipelined = len(units) >= 2
        prev = None
        for (j, u) in items:
            if u == units[0]:
                gate(j)
            scores(j, u)
            yield 0
            if not pipelined:
                pv(j, u)
                yield 3
                continue
            if prev is not None:
                pv(*prev)
                yield (3 if prev[1] == units[-1] else 1)
            prev = (j, u)
        if prev is not None:
            pv(*prev)
            yield 3

    def sgu(b):
        for g in range(4):
            gb, gr = gnext(pin=True)
            gv = gb[:].rearrange("p (a b) -> p a b", a=2)
            sc.group(pe, proj_fm(b, gv[:, 0, :], C_UB + g * 128) + proj_fm(b, gv[:, 1, :], C_GB + g * 128),
                     r_hT[b % 2] + wr_in(C_UB + g * 128) + wr_in(C_GB + g * 128), [gr])
            tu, tur = t1next()
            tb, tbr = t1next()
            sc.op(act, lambda: nc.scalar.activation(out=tu[:], in_=gv[:, 0, :], func=AF.Square, scale=GELU_A ** 0.5),
                  [gr], [tur])
            sc.op(act, lambda: nc.scalar.activation(out=tb[:], in_=gv[:, 1, :], func=AF.Tanh, scale=0.5), [gr], [tbr])
            yield
            sc.op(dve, lambda: nc.vector.scalar_tensor_tensor(out=tu[:], in0=tu[:], scalar=1.0, in1=gv[:, 0, :],
                                                              op0=ALU.add, op1=ALU.mult), [gr, tur], [tur])
            sc.op(act, lambda: nc.scalar.activation(out=tu[:], in_=tu[:], func=AF.Tanh, scale=GELU_C), [tur], [tur])
            yield
            sc.op(dve, lambda: nc.vector.scalar_tensor_tensor(out=tu[:], in0=tu[:], scalar=1.0, in1=gv[:, 0, :],
                                                              op0=ALU.add, op1=ALU.mult), [gr, tur], [tur])
            sc.op(dve, lambda: nc.vector.scalar_tensor_tensor(out=tb[:], in0=tb[:], scalar=1.0, in1=gv[:, 1, :],
                                                              op0=ALU.add, op1=ALU.mult), [gr, tbr], [tbr])
            gunpin(gr)
            sc.op(pool, lambda: nc.gpsimd.tensor_tensor(out=tu[:], in0=tu[:], in1=tb[:], op=ALU.mult), [tur, tbr], [tur])
            yield
            mb, mr = gnext()
            sc.group(pe, [(lambda i=i: mm(mb[:, i * 128:(i + 1) * 128], vn[i][:, g * 128:(g + 1) * 128], WT[:, g, :],
                                          start=True, stop=True)) for i in range(2)], r_vn, [mr])
            sc.op(dve, lambda: nc.vector.scalar_tensor_tensor(
                out=tb[:].rearrange("p (a b) -> p a b", a=2), in0=mb[:, 0:256].rearrange("p (a b) -> p a b", a=2),
                scalar=lng[:, g:g + 1], in1=bcast_mid(Cg[:, g, :], 2), op0=ALU.mult, op1=ALU.add),
                [mr, tbr, r_setup], [tbr])
            sc.op(dve, lambda: nc.vector.tensor_tensor(out=yb[b % 2][:, g, :], in0=tb[:], in1=tu[:], op=ALU.mult),
                  [tbr, tur], [r_yb[b % 2][g]])
            yield

    def phaseD(b):
        for j in range(8):
            gb, gr = gnext()
            gv = gb[:].rearrange("p (a b) -> p a b", a=2)
            sc.group(pe, proj_fm(b, gv[:, 0, :], C_GTA + j * 128) + proj_fm(b, gv[:, 1, :], C_GTB + j * 128),
                     r_hT[b % 2] + wr_in(C_GTA + j * 128) + wr_in(C_GTB + j * 128), [gr])
            sc.op(act, lambda: nc.scalar.activation(out=Gt[:, 0, :], in_=gv[:, 0, :], func=AF.Tanh, bias=hbg[:, j:j + 1],
                                                    scale=0.5), [gr, r_c["hbg"]], [r_Gt])
            sc.op(act, lambda: nc.scalar.activation(out=Gt[:, 1, :], in_=gv[:, 1, :], func=AF.Tanh,
                                                    bias=hbg[:, 8 + j:9 + j], scale=0.5), [gr, r_c["hbg"]], [r_Gt])
            yield
            pb_, pr = gnext()
            pv = pb_[:].rearrange("p (a b) -> p a b", a=2)
            fns = [(lambda ec=ec: mm(pv[:, 0, :], Wpa[:, ec, j * 128:(j + 1) * 128], ya[b % 2][:, ec, :], start=(ec == 0),
                                     stop=(ec == 3))) for ec in range(4)]
            fns += [(lambda ec=ec: mm(pv[:, 1, :], Wpb[:, ec, j * 128:(j + 1) * 128], yb[b % 2][:, ec, :], start=(ec == 0),
                                      stop=(ec == 3))) for ec in range(4)]
            sc.group(pe, fns, r_ya[b % 2] + r_yb[b % 2] + [Wres[("pa", 0, ec)] for ec in range(4)] + [Wres[("pb", 0, ec)] for ec in range(4)],
                     [pr])
            sc.op(dve, lambda: nc.vector.scalar_tensor_tensor(out=M1[:], in0=Gt[:], scalar=1.0, in1=pv[:, :, :],
                                                              op0=ALU.add, op1=ALU.mult), [pr, r_Gt], [r_M1])
            sc.op(pool, lambda: nc.gpsimd.tensor_tensor(out=merged[:, j, :], in0=M1[:, 0, :], in1=M1[:, 1, :], op=ALU.add),
                  [r_M1], [r_m[j]])
            yield

    def phaseE(b):
        slots = []
        for i in range(2):
            ti = 2 * b + i
            ft, fr, fd = fnext()
            sc.dma(sp, fd, ft[:], x_d[ti * 128:(ti + 1) * 128, :], [], [fr])
            slots.append((ft, fr, fd))
            for hf in range(2):
                gb, gr = gnext()
                sc.group(pe, [(lambda dc=dc, gb=gb: mm(gb[:, :], merged[:, dc, i * 128:(i + 1) * 128],
                                                        Wout[:, dc, hf * 512:(hf + 1) * 512], start=(dc == 0), stop=(dc == 7)))
                              for dc in range(8)], r_m + [Wres[("out", 0, dc)] for dc in range(8)], [gr])
                sc.op(dve, lambda gb=gb, ft=ft, hf=hf: nc.vector.tensor_tensor(
                    out=ft[:, hf * 512:(hf + 1) * 512], in0=gb[:, :], in1=ft[:, hf * 512:(hf + 1) * 512], op=ALU.add),
                    [gr, fr], [fr])
                yield
            sc.op(act, lambda ft=ft, i=i: nc.scalar.activation(out=PT[0][:].rearrange("p a b -> p (a b)"), in_=ft[:], func=AF.Square,
                                                              accum_out=ss2[:, i:i + 1]), [fr], [r_PT[0], r_ss2])
        sc.op(act, lambda: nc.scalar.activation(out=sb2[:], in_=ss2[:], func=AF.Sqrt, scale=1.0 / D, bias=epsc[:, 0:1]),
              [r_ss2, r_c["epsc"]], [r_sb2])
        yield
        sc.op(dve, lambda: nc.vector.reciprocal(out=rstd2[:], in_=sb2[:]), [r_sb2], [r_rstd2])
        for i in range(2):
            ti = 2 * b + i
            ft, fr, fd = slots[i]
            sc.op(dve,
                  lambda ft=ft, i=i: nc.vector.scalar_tensor_tensor(
                      out=ft[:], in0=ft[:], scalar=rstd2[:, i:i + 1], in1=fgb[:], op0=ALU.mult, op1=ALU.mult),
                  [fr, r_rstd2, r_setup], [fr])
            sc.dma(sp, fd, y_d[ti * 128:(ti + 1) * 128, :], ft[:], [fr], [])
            fr.rs[fd.sem.name] = (fd.sem, 16 * fd.cnt)
            yield

    def drain(g):
        for _ in g:
            pass

    def chain(*gens):
        for g in gens:
            yield from g

    def interleave(ga, gb_):
        for hint in ga:
            for _ in range(1 if hint is None else hint):
                next(gb_, None)
        drain(gb_)

    def wgen(jobs):
        for job in jobs:
            emit_wjob(job)
            yield

    set_wpool("all")
    for job in wjobs[:24]:
        emit_wjob(job)
    drain(normA(0))
    drain(trans(0))
    if nblocks > 1:
        drain(normA(1))
    drain(projB(0))
    set_wpool("late")
    fence(XAres)
    interleave(chain(attention(0), sgu(0)), wgen(wjobs[24:]))
    fence(XBres)
    for b in range(nblocks):
        if b + 1 < nblocks:
            if b >= 1:
                interleave(chain(trans(b + 1), projB(b + 1)), phaseE(b - 1))
            else:
                drain(chain(trans(b + 1), projB(b + 1)))
            st3 = chain(phaseD(b), normA(b + 2)) if b + 2 < nblocks else phaseD(b)
            interleave(chain(attention(b + 1), sgu(b + 1)), st3)
        else:
            if b >= 1:
                drain(phaseE(b - 1))
            drain(chain(phaseD(b), phaseE(b)))
    for fd in Fd:
        sc._wait(sp, fd.sem, 16 * fd.cnt)
    nc._sched_stats = (sc.nwait, {e.name: e.cnt for e in sc.engs})
    return nc


_CONST = {}


def _consts():
    if not _CONST:
        _CONST["ident"] = np.eye(128, dtype=np.float32)
        s = np.arange(128)[:, None]
        t = np.arange(128)[None, :]
        _CONST["trilT"] = (s <= t).astype(np.float32)
    return _CONST


def kernel(x, norm_g, w_in, b_gate, rel_bias, sgu_ln_g, sgu_ln_b, w_s, b_s, w_pa, w_pb, w_out, final_g):
    nblocks = NB
    f = lambda a: np.ascontiguousarray(np.asarray(a, dtype=np.float32))
    x = f(x)
    c = _consts()
    shared = {
        "norm_g": f(norm_g)[0], "w_in": f(w_in)[0], "b_gate": f(b_gate)[0], "rel_bias": f(rel_bias)[0],
        "sgu_ln_g": f(sgu_ln_g)[0], "sgu_ln_b": f(sgu_ln_b)[0], "w_s": f(w_s)[0], "b_s": f(b_s)[0].reshape(512),
        "w_pa": f(w_pa)[0], "w_pb": f(w_pb)[0], "w_out": f(w_out)[0], "final_g": f(final_g),
        "ident": c["ident"], "trilT": c["trilT"],
    }
    nc = build_nc(nblocks)
    in_maps = [dict(shared, x=x[i]) for i in range(8)]
    res = run_bass_kernel_spmd(nc, in_maps, core_ids=list(range(8)))
    return np.stack([np.asarray(r["y"], dtype=np.float32) for r in res.results], axis=0)
```

```python
import numpy as np
import concourse.bass as bass
import concourse.mybir as mybir
from concourse.bass_utils import run_bass_kernel_spmd

F32 = mybir.dt.float32
BF16 = mybir.dt.bfloat16
ALU = mybir.AluOpType
AF = mybir.ActivationFunctionType

D = 1024
S = 4096
DIN = 5632
NB = 16
BT = 256
EPS = 1e-6
C_Q, C_K, C_V, C_GA, C_UB, C_VB, C_GB, C_GTA, C_GTB = 0, 512, 1024, 1536, 2048, 2560, 3072, 3584, 4608
GELU_C = 0.7978845608028654
GELU_A = 0.044715


class Res:
    __slots__ = ("name", "w", "rs", "psum")

    def __init__(self, name, psum=False):
        self.name = name
        self.w = None
        self.rs = {}
        self.psum = psum


class DmaSem:
    def __init__(self, nc, name):
        self.sem = nc.alloc_semaphore(name)
        self.cnt = 0


class Eng:
    def __init__(self, nc, h, name):
        self.h = h
        self.name = name
        self.sem = nc.alloc_semaphore("s_" + name)
        self.cnt = 0
        self.waited = {}


class Sched:
    def __init__(self, nc):
        self.nc = nc
        self.pe = Eng(nc, nc.tensor, "pe")
        self.act = Eng(nc, nc.scalar, "act")
        self.dve = Eng(nc, nc.vector, "dve")
        self.pool = Eng(nc, nc.gpsimd, "pool")
        self.sp = Eng(nc, nc.sync, "sp")
        self.engs = [self.pe, self.act, self.dve, self.pool, self.sp]
        self.nwait = 0
        self.clock = {}

    def _wait(self, e, sem, val):
        if e.waited.get(sem.name, 0) >= val:
            return
        e.h.wait_ge(sem, val)
        e.waited[sem.name] = val
        self.nwait += 1
        for k, v in self.clock.get((sem.name, val), {}).items():
            if e.waited.get(k, 0) < v:
                e.waited[k] = v

    def _snap(self, e, ev):
        c = dict(e.waited)
        if e is not self.sp:
            c[e.sem.name] = e.cnt
        self.clock[(ev[0].name, ev[1])] = c

    def deps(self, e, reads, writes):
        for r in reads:
            if r.w is not None:
                sem, val = r.w
                if not (sem is e.sem and e is self.pe):
                    self._wait(e, sem, val)
            if r.psum:
                for sem, val in r.rs.values():
                    if sem is not e.sem:
                        self._wait(e, sem, val)
        for w in writes:
            if w.w is not None:
                sem, val = w.w
                if not (sem is e.sem and e is self.pe):
                    self._wait(e, sem, val)
            for sem, val in w.rs.values():
                if not (sem is e.sem and e is self.pe):
                    self._wait(e, sem, val)

    def _record(self, ev, reads, writes):
        for r in reads:
            r.rs[ev[0].name] = ev
        for w in writes:
            w.w = ev
            w.rs = {}

    def op(self, e, fn, reads=(), writes=()):
        self.deps(e, reads, writes)
        ins = fn()
        e.cnt += 1
        ins.then_inc(e.sem, 1)
        self._snap(e, (e.sem, e.cnt))
        self._record((e.sem, e.cnt), reads, writes)

    def group(self, e, fns, reads=(), writes=()):
        self.deps(e, reads, writes)
        ins = None
        for fn in fns:
            ins = fn()
        e.cnt += 1
        ins.then_inc(e.sem, 1)
        self._snap(e, (e.sem, e.cnt))
        self._record((e.sem, e.cnt), reads, writes)

    def dma(self, e, ds, out, in_, reads=(), writes=(), **kw):
        self.deps(e, reads, writes)
        ins = e.h.dma_start(out=out, in_=in_, **kw)
        ds.cnt += 1
        ins.then_inc(ds.sem, 16)
        self._snap(e, (ds.sem, 16 * ds.cnt))
        self._record((ds.sem, 16 * ds.cnt), reads, writes)

    def barrier(self):
        for e in self.engs:
            for o in self.engs:
                if o is not e and o.cnt > 0:
                    self._wait(e, o.sem, o.cnt)


def build_nc(nblocks=NB, dbg=False):
    nc = bass.Bass("TRN2", target_bir_lowering=False)
    dt = nc.dram_tensor
    x_d = dt("x", [S, D], F32, kind="ExternalInput").ap()
    ng_d = dt("norm_g", [D], F32, kind="ExternalInput").ap()
    win_d = dt("w_in", [D, DIN], F32, kind="ExternalInput").ap()
    bg_d = dt("b_gate", [2 * D], F32, kind="ExternalInput").ap()
    rb_d = dt("rel_bias", [8, 257], F32, kind="ExternalInput").ap()
    lng_d = dt("sgu_ln_g", [512], F32, kind="ExternalInput").ap()
    lnb_d = dt("sgu_ln_b", [512], F32, kind="ExternalInput").ap()
    ws_d = dt("w_s", [4, 128, 128], F32, kind="ExternalInput").ap()
    bs_d = dt("b_s", [512], F32, kind="ExternalInput").ap()
    wpa_d = dt("w_pa", [512, D], F32, kind="ExternalInput").ap()
    wpb_d = dt("w_pb", [512, D], F32, kind="ExternalInput").ap()
    wout_d = dt("w_out", [D, D], F32, kind="ExternalInput").ap()
    fg_d = dt("final_g", [D], F32, kind="ExternalInput").ap()
    id_d = dt("ident", [128, 128], F32, kind="ExternalInput").ap()
    tr_d = dt("trilT", [128, 128], F32, kind="ExternalInput").ap()
    y_d = dt("y", [S, D], F32, kind="ExternalOutput").ap()
    ext_d = dt("ext_scr", [8, 384], F32).ap()
    t2_d = dt("toe_scr", [8, 128, 256], F32).ap()

    def sb(name, shape, dtype):
        return nc.alloc_sbuf_tensor(name, shape, dtype)

    def dsz(dtype):
        return 4 if dtype == F32 else 2

    W = sb("W", [128, 8, DIN], BF16)
    Wpa = sb("Wpa", [128, 4, D], BF16)
    Wpb = sb("Wpb", [128, 4, D], BF16)
    Wout = sb("Wout", [128, 8, D], BF16)
    NF = 3
    Fs = [sb(f"F{i}", [128, D], F32) for i in range(NF)]
    hb = [sb(f"hb{i}", [128, D], BF16) for i in range(2)]
    hT = [sb(f"hT{i}", [128, 8, BT], BF16) for i in range(2)]
    qT = sb("qT", [128, 4, BT], BF16)
    kT = sb("kT", [128, 4, 768], BF16)
    Vr = sb("Vr", [128, 6, 768], BF16)
    PT = [sb("PT0", [128, 2, 512], BF16)]
    pt0 = nc.sbuf_base - 2048
    PT += [sb(f"PT{i}", [128, 2, 512], BF16) for i in (1, 2)]
    NT1 = 4
    T1 = [sb(f"T1_{i}", [128, BT], F32) for i in range(NT1)]
    vf = sb("vf", [128, 512], F32)
    vsq = sb("vsq", [128, 512], F32)
    vn = [sb(f"vn{i}", [128, 512], BF16) for i in range(2)]
    reg0 = nc.sbuf_base
    merged = sb("merged", [128, 8, BT], BF16)
    reg0 = nc.sbuf_base - 8 * BT * 2
    Gt = sb("Gt", [128, 2, BT], F32)
    M1 = sb("M1", [128, 2, BT], F32)
    ya0 = sb("ya0", [128, 4, BT], BF16)
    yb0 = sb("yb0", [128, 4, BT], BF16)
    assert nc.sbuf_base - reg0 == 12288, (nc.sbuf_base, reg0)
    _off = [reg0]

    def sbat(name, shape, dtype):
        n = int(np.prod(shape[1:])) * dsz(dtype)
        t = nc.alloc_sbuf_tensor_at(name, shape, dtype, offset=_off[0])
        _off[0] += (n + 31) // 32 * 32
        assert _off[0] <= reg0 + 12288
        return t

    ident_f = sbat("ident_f", [128, 128], F32)
    prow = sbat("prow", [28, 128], F32)
    tril_f = sbat("tril_f", [128, 128], F32)
    ws_f = sbat("ws_f", [128, 4, 128], F32)
    ws_b = sbat("ws_b", [128, 4, 128], BF16)
    bsb = sbat("bsb", [128, 512], F32)
    lnb_bc = sbat("lnb_bc", [128, 512], F32)
    lnb_hi = sbat("lnb_hi", [128, 512], BF16)
    lnb_lo = sbat("lnb_lo", [128, 512], BF16)
    ya = [ya0, sb("ya1", [128, 4, BT], BF16)]
    yb = [yb0, sb("yb1", [128, 4, BT], BF16)]
    XA = [nc.alloc_sbuf_tensor_at(f"XA{i}", [128, D], F32, offset=pt0 + 4096 * i) for i in range(2)]
    XB = [nc.alloc_sbuf_tensor_at(f"XB{i}", [128, D], F32, offset=reg0 + 4096 * i) for i in range(2)]
    fgb = sb("fgb", [128, D], F32)
    Cg = sb("Cg", [128, 4, 128], F32)
    EBX = sb("EBX", [128, 8, 256], BF16)
    WT = sb("WT", [128, 4, 128], BF16)
    identb = sb("identb", [128, 128], BF16)
    ONES3 = sb("ONES3", [128, 192], BF16)
    ng = sb("ng", [128, 8], F32)
    bg = sb("bg", [128, 16], F32)
    hbg = sb("hbg", [128, 16], F32)
    lng = sb("lng", [128, 4], F32)
    ss = sb("ss", [128, 2], F32)
    sa2 = sb("sa2", [128, 2], F32)
    rstd = sb("rstd", [128, 2], F32)
    ss2 = sb("ss2", [128, 2], F32)
    sb2 = sb("sb2", [128, 2], F32)
    rstd2 = sb("rstd2", [128, 2], F32)
    s1 = sb("s1", [128, 2], F32)
    s2 = sb("s2", [128, 2], F32)
    mu = sb("mu", [128, 2], F32)
    msq = sb("msq", [128, 2], F32)
    var = sb("var", [128, 2], F32)
    rstdv = sb("rstdv", [128, 2], F32)
    rbs = sb("rbs", [8, 257], F32)
    es = sb("es", [8, 384], F32)
    nbias = sb("nbias", [8, 1], F32)
    epsc = sb("epsc", [128, 2], F32)

    NG = 8
    Db = [nc.alloc_psum_tensor(f"Db{i}", [128, 2, 512], F32) for i in range(4)]
    Gb = [Db[i // 2][:, i % 2, :] for i in range(NG)]

    sc = Sched(nc)
    pe, act, dve, pool, sp = sc.pe, sc.act, sc.dve, sc.pool, sc.sp
    R = Res
    Gres = [R(f"G{i}", True) for i in range(NG)]
    Fres = [R(f"F{i}") for i in range(NF)]
    Fd = [DmaSem(nc, f"d_F{i}") for i in range(NF)]
    XAres = [R(f"XA{i}") for i in range(2)]
    XBres = [R(f"XB{i}") for i in range(2)]
    XAd = [DmaSem(nc, f"d_XA{i}") for i in range(2)]
    XBd = [DmaSem(nc, f"d_XB{i}") for i in range(2)]
    setup_d = DmaSem(nc, "d_setup")
    chain_d = DmaSem(nc, "d_chain")
    cnt = {"g": 0, "f": 0, "t1": 0, "s": 0, "nd": 0, "cast": 0}

    pinned = set()

    def gnext(pin=False):
        for _ in range(NG):
            i = cnt["g"] % NG
            cnt["g"] += 1
            if i not in pinned:
                if pin:
                    pinned.add(i)
                return Gb[i], Gres[i]
        raise AssertionError("all generic PSUM banks pinned")

    def gunpin(res):
        pinned.discard(Gres.index(res))

    def snext():
        for _ in range(NG):
            i = cnt["g"] % NG
            if i % 2 == 1 or i in pinned or (i + 1) in pinned:
                cnt["g"] += 1
                continue
            cnt["g"] += 2
            return Db[i // 2], [Gres[i], Gres[i + 1]]
        raise AssertionError("no free PSUM bank pair")

    def fnext():
        i = cnt["f"] % NF
        cnt["f"] += 1
        return Fs[i], Fres[i], Fd[i]

    T1res = [R(f"T1_{i}") for i in range(NT1)]

    def t1next():
        i = cnt["t1"] % NT1
        cnt["t1"] += 1
        return T1[i], T1res[i]

    mm = nc.tensor.matmul

    def bcast_mid(a, n):
        pat = [list(p) for p in a.ap]
        return bass.AP(a.tensor, a.offset, [pat[0], [0, n]] + pat[1:])

    r_setup = R("setup_in")

    def sdma(out, in_, **kw):
        sc.dma(sp, setup_d, out, in_, reads=(), writes=(r_setup,), **kw)

    sdma(ident_f[:], id_d[:, :])
    sdma(tril_f[:], tr_d[:, :])
    sdma(ws_f[:], ws_d.rearrange("g t s -> t g s"))
    sdma(bsb[:], bs_d.partition_broadcast(128))
    sdma(lnb_bc[:], lnb_d.partition_broadcast(128))
    sdma(fgb[:], fg_d.partition_broadcast(128))
    sdma(prow[0:8, :], ng_d.rearrange("(r p) -> r p", p=128))
    sdma(prow[8:24, :], bg_d.rearrange("(r p) -> r p", p=128))
    sdma(prow[24:28, :], lng_d.rearrange("(r p) -> r p", p=128))
    sdma(rbs[:], rb_d[:, :])
    r_setup.w = (setup_d.sem, 16 * setup_d.cnt)

    r_c = {k: R(k) for k in ["identb", "hbg", "ws_b", "WT", "lnb_hi", "lnb_lo", "Cg", "nb", "es", "ext", "t2",
                             "EBX", "ONES3", "Vr0", "epsc"]}
    sc.op(pool, lambda: nc.gpsimd.memset(epsc[:, 0:1], EPS), [], [r_c["epsc"]])
    sc.op(pool, lambda: nc.gpsimd.memset(epsc[:, 1:2], 4.0 * EPS), [r_c["epsc"]], [r_c["epsc"]])
    sc.op(dve, lambda: nc.vector.tensor_copy(out=identb[:], in_=ident_f[:]), [r_setup], [r_c["identb"]])
    gb, gr = gnext()
    sc.group(pe, [lambda: mm(gb[:, 0:28], prow[0:28, :], ident_f[0:28, 0:28], start=True, stop=True)], [r_setup], [gr])
    r_c["pcols"] = R("pcols")
    sc.op(dve, lambda: nc.vector.tensor_copy(out=ng[:], in_=gb[:, 0:8]), [gr], [r_c["pcols"]])
    sc.op(dve, lambda: nc.vector.tensor_copy(out=lng[:], in_=gb[:, 24:28]), [gr], [r_c["pcols"]])
    sc.op(dve, lambda: nc.vector.tensor_scalar(out=hbg[:], in0=gb[:, 8:24], scalar1=0.5, scalar2=None, op0=ALU.mult),
          [gr], [r_c["hbg"]])
    sc.op(act, lambda: nc.scalar.copy(out=ws_b[:], in_=ws_f[:]), [r_setup], [r_c["ws_b"]])
    gb, gr = gnext()
    gbv = gb[:].bitcast(BF16).rearrange("p (a b) -> p a b", a=8)
    sc.group(pe, [(lambda g=g: nc.tensor.transpose(out=gbv[:, g, :], in_=ws_b[:, g, :], identity=identb[:]))
                  for g in range(4)], [r_c["ws_b"], r_c["identb"]], [gr])
    sc.op(dve, lambda: nc.vector.tensor_tensor(out=WT[:], in0=gbv[:, 0:4, :], in1=bcast_mid(tril_f[:], 4), op=ALU.mult),
          [gr, r_setup], [r_c["WT"]])
    sc.op(dve, lambda: nc.vector.tensor_copy(out=lnb_hi[:], in_=lnb_bc[:]), [r_setup], [r_c["lnb_hi"]])
    sc.op(dve, lambda: nc.vector.tensor_tensor(out=lnb_lo[:], in0=lnb_bc[:], in1=lnb_hi[:], op=ALU.subtract),
          [r_setup, r_c["lnb_hi"]], [r_c["lnb_lo"]])
    gb, gr = gnext()
    gcv = gb[:].rearrange("p (a b) -> p a b", a=4)
    fns = []
    for g in range(4):
        fns.append(lambda g=g: mm(gcv[:, g, :], lnb_hi[:, g * 128:(g + 1) * 128], WT[:, g, :], start=True, stop=False))
        fns.append(lambda g=g: mm(gcv[:, g, :], lnb_lo[:, g * 128:(g + 1) * 128], WT[:, g, :], start=False, stop=True))
    sc.group(pe, fns, [r_c["lnb_hi"], r_c["lnb_lo"], r_c["WT"]], [gr])
    sc.op(dve, lambda: nc.vector.tensor_tensor(out=Cg[:], in0=gcv[:, :, :],
                                               in1=bsb[:].rearrange("p (a b) -> p a b", a=4), op=ALU.add),
          [gr, r_setup], [r_c["Cg"]])
    sc.op(dve, lambda: nc.vector.tensor_scalar(out=nbias[:], in0=rbs[:, 256:257], scalar1=-1.0, scalar2=None,
                                               op0=ALU.mult), [r_setup], [r_c["nb"]])
    sc.op(pool, lambda: nc.gpsimd.memset(es[:], 1.0), [], [r_c["es"]])
    sc.op(act, lambda: nc.scalar.activation(out=es[:, 0:257], in_=rbs[:], func=AF.Exp, bias=nbias[:, 0:1], scale=1.0),
          [r_setup, r_c["nb"], r_c["es"]], [r_c["es"]])
    sc.dma(sp, chain_d, ext_d[:, :], es[:], [r_c["es"]], [r_c["ext"]])
    sc.dma(sp, chain_d, t2_d[:, :, :], bass.AP(ext_d.tensor, 128, [[384, 8], [-1, 128], [1, 256]]),
           [r_c["ext"]], [r_c["t2"]])
    for half in range(2):
        ft, fr, fd = fnext()
        sc.dma(sp, fd, ft[:].rearrange("p (a b) -> p a b", a=4),
               t2_d[4 * half:4 * half + 4, :, :].rearrange("h k j -> k h j"), [r_c["t2"]], [fr])
        sc.op(dve, lambda ft=ft, half=half: nc.vector.tensor_copy(
            out=EBX[:, 4 * half:4 * half + 4, 0:256], in_=ft[:].rearrange("p (a b) -> p a b", a=4)),
            [fr], [r_c["EBX"]])
    sc.op(pool, lambda: nc.gpsimd.memset(EBX[64:128, :, 0:64], 0.0), [r_c["EBX"]], [r_c["EBX"]])
    sc.op(pool, lambda: nc.gpsimd.memset(ONES3[:], 0.0), [], [r_c["ONES3"]])
    sc.op(pool, lambda: nc.gpsimd.memset(ONES3[:, 64:128], 1.0), [r_c["ONES3"]], [r_c["ONES3"]])
    sc.op(pool, lambda: nc.gpsimd.memset(Vr[:], 0.0), [], [r_c["Vr0"]])
    sc.barrier()

    Wres = {}
    wjobs = []
    for p in range(6):
        c0 = 1024 * p
        c1 = min(c0 + 1024, DIN)
        for kc in range(8):
            wjobs.append(("in", p, kc, win_d[kc * 128:(kc + 1) * 128, c0:c1], W[:, kc, c0:c1], c1 - c0, ng[:, kc:kc + 1]))
    for kc in range(4):
        wjobs.append(("pa", 0, kc, wpa_d[kc * 128:(kc + 1) * 128, :], Wpa[:, kc, :], D, 0.5))
    for kc in range(4):
        wjobs.append(("pb", 0, kc, wpb_d[kc * 128:(kc + 1) * 128, :], Wpb[:, kc, :], D, 0.25))
    for kc in range(8):
        wjobs.append(("out", 0, kc, wout_d[kc * 128:(kc + 1) * 128, :], Wout[:, kc, :], D, 0.5))

    wpool = {"slots": None, "i": 0}

    def set_wpool(kind):
        base = [(Fs[i], Fres[i], Fd[i]) for i in range(NF)]
        xa = [(XA[i], XAres[i], XAd[i]) for i in range(2)]
        xb = [(XB[i], XBres[i], XBd[i]) for i in range(2)]
        wpool["slots"] = {"all": base + xa + xb, "late": base + xb, "base": base}[kind]

    def fence(ress):
        for e in (pe, act, dve, pool):
            sc.deps(e, [], ress)

    def emit_wjob(job):
        name, p, kc, src, dst, n, scal = job
        ft, fr, fd = wpool["slots"][wpool["i"] % len(wpool["slots"])]
        wpool["i"] += 1
        sc.dma(sp, fd, ft[:, 0:n], src, [], [fr])
        res = R(f"W{name}{p}_{kc}")
        Wres[(name, p, kc)] = res
        k = cnt["cast"]
        cnt["cast"] += 1
        if k % 2 == 0:
            sc.op(act, lambda: nc.scalar.mul(out=dst, in_=ft[:, 0:n], mul=scal), [fr, r_setup], [res])
        else:
            sc.op(dve, lambda: nc.vector.tensor_scalar(out=dst, in0=ft[:, 0:n], scalar1=scal, scalar2=None, op0=ALU.mult),
                  [fr, r_setup], [res])

    def wr_in(c0, n=128):
        p0, p1 = c0 // 1024, (c0 + n - 1) // 1024
        return [Wres[("in", p, kc)] for p in range(p0, p1 + 1) for kc in range(8)]

    r_hb = [R("hb0"), R("hb1")]
    r_junk = R("junk")
    r_ss, r_sa2, r_rstd = R("ss"), R("sa2"), R("rstd")
    r_hT = [[R(f"hT{p}_{i}") for i in range(2)] for p in range(2)]
    r_q = [R(f"q{j}") for j in range(4)]
    r_k = [[R(f"k{s}_{j}") for j in range(4)] for s in range(3)]
    r_V = [R(f"V{s}") for s in range(6)]
    r_PT = [R(f"PT{u}") for u in range(3)]
    r_vf, r_vsq = R("vf"), R("vsq")
    r_vn = [R("vn0"), R("vn1")]
    r_s1, r_s2, r_mu, r_msq, r_var, r_rstdv = R("s1"), R("s2"), R("mu"), R("msq"), R("var"), R("rstdv")
    r_ya = [[R(f"ya{p}_{j}") for j in range(4)] for p in range(2)]
    r_yb = [[R(f"yb{p}_{g}") for g in range(4)] for p in range(2)]
    r_Gt, r_M1 = R("Gt"), R("M1")
    r_m = [R(f"m{j}") for j in range(8)]
    r_ss2, r_sb2, r_rstd2 = R("ss2"), R("sb2"), R("rstd2")
    xin = {}

    def normA(b):
        for i in range(2):
            ti = 2 * b + i
            ft, fr, fd = fnext()
            sc.dma(sp, fd, ft[:], x_d[ti * 128:(ti + 1) * 128, :], [], [fr])
            xin[ti] = (ft, fr)
            sc.op(act, lambda ft=ft, i=i: nc.scalar.activation(out=hb[i][:], in_=ft[:], func=AF.Square,
                                                              accum_out=ss[:, i:i + 1]),
                  [fr], [r_hb[i], r_ss])
        yield
        sc.op(act, lambda: nc.scalar.activation(out=sa2[:], in_=ss[:], func=AF.Sqrt, scale=1.0 / D, bias=epsc[:, 0:1]),
              [r_ss, r_c["epsc"]], [r_sa2])
        yield
        sc.op(dve, lambda: nc.vector.reciprocal(out=rstd[:], in_=sa2[:]), [r_sa2], [r_rstd])
        yield
        for i in range(2):
            ft, fr = xin[2 * b + i]
            sc.op(act, lambda ft=ft, i=i: nc.scalar.mul(out=hb[i][:], in_=ft[:], mul=rstd[:, i:i + 1]),
                  [fr, r_rstd], [r_hb[i]])
        yield

    def trans(b):
        for i in range(2):
            for half in range(2):
                gb, gr = gnext()
                gv = gb[:].rearrange("p (a b) -> p a b", a=4)
                sc.group(pe, [(lambda c=c, gv=gv: mm(gv[:, c, :], hb[i][:, (4 * half + c) * 128:(4 * half + c + 1) * 128],
                                                     identb[:], start=True, stop=True)) for c in range(4)],
                         [r_hb[i]], [gr])
                if half == 0:
                    sc.op(dve, lambda gv=gv: nc.vector.tensor_copy(out=hT[b % 2][:, 0:4, i * 128:(i + 1) * 128], in_=gv[:, :, :]),
                          [gr], [r_hT[b % 2][i]])
                else:
                    sc.op(act, lambda gv=gv: nc.scalar.copy(out=hT[b % 2][:, 4:8, i * 128:(i + 1) * 128], in_=gv[:, :, :]),
                          [gr], [r_hT[b % 2][i]])
                yield

    def proj_fm(b, out_ap, c0):
        return [(lambda kc=kc: mm(out_ap, W[:, kc, c0:c0 + 128], hT[b % 2][:, kc, :], start=(kc == 0), stop=(kc == 7)))
                for kc in range(8)]

    def projB(b):
        seg = b % 3
        for i in range(2):
            gb, gr = gnext(pin=True)
            sc.group(pe, [(lambda kc=kc, gb=gb, i=i: mm(gb[:, :], hT[b % 2][:, kc, i * 128:(i + 1) * 128], W[:, kc, C_VB:C_VB + 512],
                                                         start=(kc == 0), stop=(kc == 7))) for kc in range(8)],
                     [r_hT[b % 2][i]] + wr_in(C_VB, 512), [gr])
            sc.op(act, lambda gb=gb: nc.scalar.activation(out=vsq[:], in_=gb[:, :], func=AF.Square, scale=GELU_A ** 0.5),
                  [gr], [r_vsq])
            yield
            sc.op(dve, lambda gb=gb: nc.vector.scalar_tensor_tensor(out=vsq[:], in0=vsq[:], scalar=1.0, in1=gb[:, :],
                                                                    op0=ALU.add, op1=ALU.mult), [gr, r_vsq], [r_vsq])
            sc.op(act, lambda: nc.scalar.activation(out=vsq[:], in_=vsq[:], func=AF.Tanh, scale=GELU_C), [r_vsq], [r_vsq])
            yield
            sc.op(dve, lambda gb=gb: nc.vector.scalar_tensor_tensor(out=vf[:], in0=vsq[:], scalar=1.0, in1=gb[:, :],
                                                                    op0=ALU.add, op1=ALU.mult), [gr, r_vsq], [r_vf])
            gunpin(gr)
            sc.op(dve, lambda: nc.vector.tensor_reduce(out=s1[:, 0:1], in_=vf[:], axis=mybir.AxisListType.X, op=ALU.add),
                  [r_vf], [r_s1])
            sc.op(act, lambda: nc.scalar.activation(out=vsq[:], in_=vf[:], func=AF.Square, accum_out=s2[:, 0:1]),
                  [r_vf, r_vsq], [r_vsq, r_s2])
            yield
            sc.op(dve, lambda: nc.vector.tensor_scalar(out=mu[:, 0:1], in0=s1[:, 0:1], scalar1=1.0 / 512, scalar2=None,
                                                       op0=ALU.mult), [r_s1], [r_mu])
            sc.op(dve, lambda: nc.vector.scalar_tensor_tensor(out=msq[:, 0:1], in0=s1[:, 0:1], scalar=-1.0 / (512.0 * 512.0),
                                                              in1=s1[:, 0:1], op0=ALU.mult, op1=ALU.mult), [r_s1], [r_msq])
            sc.op(dve, lambda: nc.vector.scalar_tensor_tensor(out=var[:, 0:1], in0=s2[:, 0:1], scalar=1.0 / 512,
                                                              in1=msq[:, 0:1], op0=ALU.mult, op1=ALU.add),
                  [r_s2, r_msq], [r_var])
            sc.op(act, lambda: nc.scalar.activation(out=var[:, 0:1], in_=var[:, 0:1], func=AF.Sqrt, scale=1.0,
                                                    bias=epsc[:, 1:2]), [r_var, r_c["epsc"]], [r_var])
            yield
            sc.op(dve, lambda: nc.vector.reciprocal(out=rstdv[:, 0:1], in_=var[:, 0:1]), [r_var], [r_rstdv])
            sc.op(dve, lambda i=i: nc.vector.tensor_scalar(out=vn[i][:], in0=vf[:], scalar1=mu[:, 0:1],
                                                           scalar2=rstdv[:, 0:1], op0=ALU.subtract, op1=ALU.mult),
                  [r_vf, r_mu, r_rstdv], [r_vn[i]])
            yield
        for j in range(4):
            gb, gr = gnext()
            gv = gb[:].rearrange("p (a b) -> p a b", a=2)
            sc.group(pe, proj_fm(b, gv[:, 0, :], C_Q + j * 128) + proj_fm(b, gv[:, 1, :], C_K + j * 128),
                     r_hT[b % 2] + wr_in(C_Q + j * 128) + wr_in(C_K + j * 128), [gr])
            sc.op(act, lambda gv=gv, j=j: nc.scalar.copy(out=qT[:, j, :], in_=gv[:, 0, :]), [gr], [r_q[j]])
            sc.op(dve, lambda gv=gv, j=j: nc.vector.tensor_copy(out=kT[:, j, seg * 256:(seg + 1) * 256], in_=gv[:, 1, :]),
                  [gr], [r_k[seg][j]])
            yield
        for i in range(2):
            ti = 2 * b + i
            slot = ti % 6
            gb, gr = gnext()
            sc.group(pe, [(lambda kc=kc, gb=gb, i=i: mm(gb[:, :], hT[b % 2][:, kc, i * 128:(i + 1) * 128], W[:, kc, C_V:C_V + 512],
                                                         start=(kc == 0), stop=(kc == 7))) for kc in range(8)],
                     [r_hT[b % 2][i]] + wr_in(C_V, 512), [gr])
            vdst = Vr[:, slot, :].rearrange("p (j c) -> p j c", j=4)
            sc.op(act, lambda gb=gb, vdst=vdst: nc.scalar.copy(
                out=vdst[:, :, 0:64], in_=gb[:, :].rearrange("p (j c) -> p j c", j=4)[:, :, 0:64]), [gr], [r_V[slot]])
            sc.op(dve, lambda gb=gb, vdst=vdst: nc.vector.tensor_copy(
                out=vdst[:, :, 128:192], in_=gb[:, :].rearrange("p (j c) -> p j c", j=4)[:, :, 64:128]), [gr], [r_V[slot]])
            yield
    TILES = {0: (0, 256, 0, 128), 1: (0, 0, 0, 256), 2: (1, 0, 0, 256), 3: (1, 256, 0, 256),
             4: (2, 0, 0, 256), 5: (2, 256, 128, 256)}
    UNIT_TILES = {0: (1, 0), 1: (2, 3), 2: (4, 5)}
    UNIT_W = {0: 384, 1: 512, 2: 384}

    def attention(b):
        units = [u for u in (2, 1, 0) if 2 * b - 4 + UNIT_TILES[u][0] >= 0 and 2 * b - 4 + UNIT_TILES[u][1] >= 0]
        items = [(j, u) for j in range(4) for u in units]
        tgs = {}
        nd = {}

        def nd_banks(j):
            if j not in nd:
                nb_, nr_ = gnext(pin=True)
                db_, dr_ = gnext(pin=True)
                nd[j] = (nb_, nr_, db_, dr_)
            return nd[j]

        def gate(j):
            gb, gr = gnext()
            sc.group(pe, proj_fm(b, gb[:, 0:256], C_GA + j * 128), r_hT[b % 2] + wr_in(C_GA + j * 128), [gr])
            tg, tgr = t1next()
            sc.op(act, lambda: nc.scalar.activation(out=tg[:], in_=gb[:, 0:256], func=AF.Tanh, scale=0.5), [gr], [tgr])
            sc.op(dve, lambda: nc.vector.scalar_tensor_tensor(out=tg[:], in0=tg[:], scalar=1.0, in1=gb[:, 0:256],
                                                              op0=ALU.add, op1=ALU.mult), [gr, tgr], [tgr])
            tgs[j] = (tg, tgr)

        def scores(j, u):
            sbuf, sres2 = snext()
            fns = []
            rd = [r_q[j]]
            for t in UNIT_TILES[u]:
                _, uoff, qlo, qhi = TILES[t]
                gt = 2 * b - 4 + t
                kpos = (gt * 128) % 768
                rd.append(r_k[kpos // 256][j])
                for r in range(2):
                    fns.append(lambda r=r, uoff=uoff, qlo=qlo, qhi=qhi, kpos=kpos: mm(
                        sbuf[:, r, uoff:uoff + qhi - qlo], kT[64 * r:64 * r + 64, j, kpos:kpos + 128],
                        qT[64 * r:64 * r + 64, j, qlo:qhi], start=True, stop=True))
            sc.group(pe, fns, rd, sres2)
            w = UNIT_W[u]
            sc.op(act, lambda: nc.scalar.activation(out=PT[u][:, :, 0:w], in_=sbuf[:, :, 0:w], func=AF.Exp, scale=0.125),
                  sres2, [r_PT[u]])
            if u == 0:
                sc.op(dve, lambda: nc.vector.memset(PT[0][0:64, :, 320:384], 0.0), [r_PT[0]], [r_PT[0]])
                sc.op(dve, lambda: nc.vector.memset(PT[0][0:64, :, 192:256], 0.0), [r_PT[0]], [r_PT[0]])
            elif u == 1:
                sc.op(dve, lambda: nc.vector.tensor_tensor(out=PT[1][:, :, 256:384], in0=PT[1][:, :, 256:384],
                                                           in1=EBX[:, 2 * j:2 * j + 2, 128:256], op=ALU.mult),
                      [r_PT[1]], [r_PT[1]])
            else:
                sc.op(dve, lambda: nc.vector.tensor_tensor(out=PT[2][:, :, 0:256], in0=PT[2][:, :, 0:256],
                                                           in1=EBX[:, 2 * j:2 * j + 2, 0:256], op=ALU.mult),
                      [r_PT[2]], [r_PT[2]])
                sc.op(dve, lambda: nc.vector.tensor_tensor(out=PT[2][:, :, 256:384], in0=PT[2][:, :, 256:384],
                                                           in1=EBX[:, 2 * j:2 * j + 2, 0:128], op=ALU.mult),
                      [r_PT[2]], [r_PT[2]])

        def pv(j, u):
            numb, numr, denb, denr = nd_banks(j)
            first = (u == units[0])
            last = (u == units[-1])
            tiles = sorted(UNIT_TILES[u])
            fns = []
            n = 2 * len(tiles)
            k = 0
            for t in tiles:
                _, uoff, qlo, qhi = TILES[t]
                slot = (2 * b - 4 + t) % 6
                for r in range(2):
                    st = first and k == 0
                    sp_ = last and k == n - 1
                    lhs_v = Vr[:, slot, 192 * j + 64 * r:192 * j + 64 * r + 128]
                    lhs_1 = ONES3[:, 64:192] if r == 0 else ONES3[:, 0:128]
                    rhs = PT[u][:, r, uoff:uoff + qhi - qlo]
                    fns.append(lambda lhs_v=lhs_v, rhs=rhs, qlo=qlo, qhi=qhi, st=st, sp_=sp_: mm(
                        numb[:, qlo:qhi], lhs_v, rhs, start=st, stop=sp_))
                    fns.append(lambda lhs_1=lhs_1, rhs=rhs, qlo=qlo, qhi=qhi, st=st, sp_=sp_: mm(
                        denb[:, qlo:qhi], lhs_1, rhs, start=st, stop=sp_))
                    k += 1
            sc.group(pe, fns, [r_PT[u]] + [r_V[(2 * b - 4 + t) % 6] for t in tiles], [numr, denr])
            if last:
                tg, tgr = tgs[j]
                rd_, rdr = t1next()
                sc.op(dve, lambda: nc.vector.reciprocal(out=rd_[:], in_=denb[:, 0:256]), [denr], [rdr])
                sc.op(dve, lambda: nc.vector.tensor_tensor(out=rd_[:], in0=numb[:, 0:256], in1=rd_[:], op=ALU.mult),
                      [numr, rdr], [rdr])
                sc.op(pool, lambda: nc.gpsimd.tensor_tensor(out=ya[b % 2][:, j, :], in0=rd_[:], in1=tg[:], op=ALU.mult),
                      [rdr, tgr], [r_ya[b % 2][j]])
                gunpin(numr)
                gunpin(denr)

        assert units[0] == 2
        pipelined = len(units) >= 2
        prev = None
        for (j, u) in items:
            if u == units[0]:
                gate(j)
            scores(j, u)
            yield 0
            if not pipelined:
                pv(j, u)
                yield 3
                continue
            if prev is not None:
                pv(*prev)
                yield (3 if prev[1] == units[-1] else 1)
            prev = (j, u)
        if prev is not None:
            pv(*prev)
            yield 3

    def sgu(b):
        for g in range(4):
            gb, gr = gnext(pin=True)
            gv = gb[:].rearrange("p (a b) -> p a b", a=2)
            sc.group(pe, proj_fm(b, gv[:, 0, :], C_UB + g * 128) + proj_fm(b, gv[:, 1, :], C_GB + g * 128),
                     r_hT[b % 2] + wr_in(C_UB + g * 128) + wr_in(C_GB + g * 128), [gr])
            tu, tur = t1next()
            tb, tbr = t1next()
            sc.op(act, lambda: nc.scalar.activation(out=tu[:], in_=gv[:, 0, :], func=AF.Square, scale=GELU_A ** 0.5),
                  [gr], [tur])
            sc.op(act, lambda: nc.scalar.activation(out=tb[:], in_=gv[:, 1, :], func=AF.Tanh, scale=0.5), [gr], [tbr])
            yield
            sc.op(dve, lambda: nc.vector.scalar_tensor_tensor(out=tu[:], in0=tu[:], scalar=1.0, in1=gv[:, 0, :],
                                                              op0=ALU.add, op1=ALU.mult), [gr, tur], [tur])
            sc.op(act, lambda: nc.scalar.activation(out=tu[:], in_=tu[:], func=AF.Tanh, scale=GELU_C), [tur], [tur])
            yield
            sc.op(dve, lambda: nc.vector.scalar_tensor_tensor(out=tu[:], in0=tu[:], scalar=1.0, in1=gv[:, 0, :],
                                                              op0=ALU.add, op1=ALU.mult), [gr, tur], [tur])
            sc.op(dve, lambda: nc.vector.scalar_tensor_tensor(out=tb[:], in0=tb[:], scalar=1.0, in1=gv[:, 1, :],
                                                              op0=ALU.add, op1=ALU.mult), [gr, tbr], [tbr])
            gunpin(gr)
            sc.op(pool, lambda: nc.gpsimd.tensor_tensor(out=tu[:], in0=tu[:], in1=tb[:], op=ALU.mult), [tur, tbr], [tur])
            yield
            mb, mr = gnext()
            sc.group(pe, [(lambda i=i: mm(mb[:, i * 128:(i + 1) * 128], vn[i][:, g * 128:(g + 1) * 128], WT[:, g, :],
                                          start=True, stop=True)) for i in range(2)], r_vn, [mr])
            sc.op(dve, lambda: nc.vector.scalar_tensor_tensor(
                out=tb[:].rearrange("p (a b) -> p a b", a=2), in0=mb[:, 0:256].rearrange("p (a b) -> p a b", a=2),
                scalar=lng[:, g:g + 1], in1=bcast_mid(Cg[:, g, :], 2), op0=ALU.mult, op1=ALU.add),
                [mr, tbr, r_setup], [tbr])
            sc.op(dve, lambda: nc.vector.tensor_tensor(out=yb[b % 2][:, g, :], in0=tb[:], in1=tu[:], op=ALU.mult),
                  [tbr, tur], [r_yb[b % 2][g]])
            yield

    def phaseD(b):
        for j in range(8):
            gb, gr = gnext()
            gv = gb[:].rearrange("p (a b) -> p a b", a=2)
            sc.group(pe, proj_fm(b, gv[:, 0, :], C_GTA + j * 128) + proj_fm(b, gv[:, 1, :], C_GTB + j * 128),
                     r_hT[b % 2] + wr_in(C_GTA + j * 128) + wr_in(C_GTB + j * 128), [gr])
            sc.op(act, lambda: nc.scalar.activation(out=Gt[:, 0, :], in_=gv[:, 0, :], func=AF.Tanh, bias=hbg[:, j:j + 1],
                                                    scale=0.5), [gr, r_c["hbg"]], [r_Gt])
            sc.op(act, lambda: nc.scalar.activation(out=Gt[:, 1, :], in_=gv[:, 1, :], func=AF.Tanh,
                                                    bias=hbg[:, 8 + j:9 + j], scale=0.5), [gr, r_c["hbg"]], [r_Gt])
            yield
            pb_, pr = gnext()
            pv = pb_[:].rearrange("p (a b) -> p a b", a=2)
            fns = [(lambda ec=ec: mm(pv[:, 0, :], Wpa[:, ec, j * 128:(j + 1) * 128], ya[b % 2][:, ec, :], start=(ec == 0),
                                     stop=(ec == 3))) for ec in range(4)]
            fns += [(lambda ec=ec: mm(pv[:, 1, :], Wpb[:, ec, j * 128:(j + 1) * 128], yb[b % 2][:, ec, :], start=(ec == 0),
                                      stop=(ec == 3))) for ec in range(4)]
            sc.group(pe, fns, r_ya[b % 2] + r_yb[b % 2] + [Wres[("pa", 0, ec)] for ec in range(4)] + [Wres[("pb", 0, ec)] for ec in range(4)],
                     [pr])
            sc.op(dve, lambda: nc.vector.scalar_tensor_tensor(out=M1[:], in0=Gt[:], scalar=1.0, in1=pv[:, :, :],
                                                              op0=ALU.add, op1=ALU.mult), [pr, r_Gt], [r_M1])
            sc.op(pool, lambda: nc.gpsimd.tensor_tensor(out=merged[:, j, :], in0=M1[:, 0, :], in1=M1[:, 1, :], op=ALU.add),
                  [r_M1], [r_m[j]])
            yield

    def phaseE(b):
        slots = []
        for i in range(2):
            ti = 2 * b + i
            ft, fr, fd = fnext()
            sc.dma(sp, fd, ft[:], x_d[ti * 128:(ti + 1) * 128, :], [], [fr])
            slots.append((ft, fr, fd))
            for hf in range(2):
                gb, gr = gnext()
                sc.group(pe, [(lambda dc=dc, gb=gb: mm(gb[:, :], merged[:, dc, i * 128:(i + 1) * 128],
                                                        Wout[:, dc, hf * 512:(hf + 1) * 512], start=(dc == 0), stop=(dc == 7)))
                              for dc in range(8)], r_m + [Wres[("out", 0, dc)] for dc in range(8)], [gr])
                sc.op(dve, lambda gb=gb, ft=ft, hf=hf: nc.vector.tensor_tensor(
                    out=ft[:, hf * 512:(hf + 1) * 512], in0=gb[:, :], in1=ft[:, hf * 512:(hf + 1) * 512], op=ALU.add),
                    [gr, fr], [fr])
                yield
            sc.op(act, lambda ft=ft, i=i: nc.scalar.activation(out=PT[0][:].rearrange("p a b -> p (a b)"), in_=ft[:], func=AF.Square,
                                                              accum_out=ss2[:, i:i + 1]), [fr], [r_PT[0], r_ss2])
        sc.op(act, lambda: nc.scalar.activation(out=sb2[:], in_=ss2[:], func=AF.Sqrt, scale=1.0 / D, bias=epsc[:, 0:1]),
              [r_ss2, r_c["epsc"]], [r_sb2])
        yield
        sc.op(dve, lambda: nc.vector.reciprocal(out=rstd2[:], in_=sb2[:]), [r_sb2], [r_rstd2])
        for i in range(2):
            ti = 2 * b + i
            ft, fr, fd = slots[i]
            sc.op(dve,
                  lambda ft=ft, i=i: nc.vector.scalar_tensor_tensor(
                      out=ft[:], in0=ft[:], scalar=rstd2[:, i:i + 1], in1=fgb[:], op0=ALU.mult, op1=ALU.mult),
                  [fr, r_rstd2, r_setup], [fr])
            sc.dma(sp, fd, y_d[ti * 128:(ti + 1) * 128, :], ft[:], [fr], [])
            fr.rs[fd.sem.name] = (fd.sem, 16 * fd.cnt)
            yield

    def drain(g):
        for _ in g:
            pass

    def chain(*gens):
        for g in gens:
            yield from g

    def interleave(ga, gb_):
        for hint in ga:
            for _ in range(1 if hint is None else hint):
                next(gb_, None)
        drain(gb_)

    def wgen(jobs):
        for job in jobs:
            emit_wjob(job)
            yield

    set_wpool("all")
    for job in wjobs[:24]:
        emit_wjob(job)
    drain(normA(0))
    drain(trans(0))
    if nblocks > 1:
        drain(normA(1))
    drain(projB(0))
    set_wpool("late")
    fence(XAres)
    interleave(chain(attention(0), sgu(0)), wgen(wjobs[24:]))
    fence(XBres)
    for b in range(nblocks):
        if b + 1 < nblocks:
            if b >= 1:
                interleave(chain(trans(b + 1), projB(b + 1)), phaseE(b - 1))
            else:
                drain(chain(trans(b + 1), projB(b + 1)))
            st3 = chain(phaseD(b), normA(b + 2)) if b + 2 < nblocks else phaseD(b)
            interleave(chain(attention(b + 1), sgu(b + 1)), st3)
        else:
            if b >= 1:
                drain(phaseE(b - 1))
            drain(chain(phaseD(b), phaseE(b)))
    for fd in Fd:
        sc._wait(sp, fd.sem, 16 * fd.cnt)
    nc._sched_stats = (sc.nwait, {e.name: e.cnt for e in sc.engs})
    return nc


_CONST = {}


def _consts():
    if not _CONST:
        _CONST["ident"] = np.eye(128, dtype=np.float32)
        s = np.arange(128)[:, None]
        t = np.arange(128)[None, :]
        _CONST["trilT"] = (s <= t).astype(np.float32)
    return _CONST


def kernel(x, norm_g, w_in, b_gate, rel_bias, sgu_ln_g, sgu_ln_b, w_s, b_s, w_pa, w_pb, w_out, final_g):
    nblocks = NB
    f = lambda a: np.ascontiguousarray(np.asarray(a, dtype=np.float32))
    x = f(x)
    c = _consts()
    shared = {
        "norm_g": f(norm_g)[0], "w_in": f(w_in)[0], "b_gate": f(b_gate)[0], "rel_bias": f(rel_bias)[0],
        "sgu_ln_g": f(sgu_ln_g)[0], "sgu_ln_b": f(sgu_ln_b)[0], "w_s": f(w_s)[0], "b_s": f(b_s)[0].reshape(512),
        "w_pa": f(w_pa)[0], "w_pb": f(w_pb)[0], "w_out": f(w_out)[0], "final_g": f(final_g),
        "ident": c["ident"], "trilT": c["trilT"],
    }
    nc = build_nc(nblocks)
    in_maps = [dict(shared, x=x[i]) for i in range(8)]
    res = run_bass_kernel_spmd(nc, in_maps, core_ids=list(range(8)))
    return np.stack([np.asarray(r["y"], dtype=np.float32) for r in res.results], axis=0)
```

```python
import numpy as np
import concourse.bass as bass
import concourse.mybir as mybir
from concourse.bass_utils import run_bass_kernel_spmd

F32 = mybir.dt.float32
BF16 = mybir.dt.bfloat16
ALU = mybir.AluOpType
AF = mybir.ActivationFunctionType

D = 1024
S = 4096
DIN = 5632
NB = 16
BT = 256
EPS = 1e-6
C_Q, C_K, C_V, C_GA, C_UB, C_VB, C_GB, C_GTA, C_GTB = 0, 512, 1024, 1536, 2048, 2560, 3072, 3584, 4608
GELU_C = 0.7978845608028654
GELU_A = 0.044715


class Res:
    __slots__ = ("name", "w", "rs", "psum")

    def __init__(self, name, psum=False):
        self.name = name
        self.w = None
        self.rs = {}
        self.psum = psum


class DmaSem:
    def __init__(self, nc, name):
        self.sem = nc.alloc_semaphore(name)
        self.cnt = 0


class Eng:
    def __init__(self, nc, h, name):
        self.h = h
        self.name = name
        self.sem = nc.alloc_semaphore("s_" + name)
        self.cnt = 0
        self.waited = {}


class Sched:
    def __init__(self, nc):
        self.nc = nc
        self.pe = Eng(nc, nc.tensor, "pe")
        self.act = Eng(nc, nc.scalar, "act")
        self.dve = Eng(nc, nc.vector, "dve")
        self.pool = Eng(nc, nc.gpsimd, "pool")
        self.sp = Eng(nc, nc.sync, "sp")
        self.engs = [self.pe, self.act, self.dve, self.pool, self.sp]
        self.nwait = 0
        self.clock = {}

    def _wait(self, e, sem, val):
        if e.waited.get(sem.name, 0) >= val:
            return
        e.h.wait_ge(sem, val)
        e.waited[sem.name] = val
        self.nwait += 1
        for k, v in self.clock.get((sem.name, val), {}).items():
            if e.waited.get(k, 0) < v:
                e.waited[k] = v

    def _snap(self, e, ev):
        c = dict(e.waited)
        if e is not self.sp:
            c[e.sem.name] = e.cnt
        self.clock[(ev[0].name, ev[1])] = c

    def deps(self, e, reads, writes):
        for r in reads:
            if r.w is not None:
                sem, val = r.w
                if not (sem is e.sem and e is self.pe):
                    self._wait(e, sem, val)
            if r.psum:
                for sem, val in r.rs.values():
                    if sem is not e.sem:
                        self._wait(e, sem, val)
        for w in writes:
            if w.w is not None:
                sem, val = w.w
                if not (sem is e.sem and e is self.pe):
                    self._wait(e, sem, val)
            for sem, val in w.rs.values():
                if not (sem is e.sem and e is self.pe):
                    self._wait(e, sem, val)

    def _record(self, ev, reads, writes):
        for r in reads:
            r.rs[ev[0].name] = ev
        for w in writes:
            w.w = ev
            w.rs = {}

    def op(self, e, fn, reads=(), writes=()):
        self.deps(e, reads, writes)
        ins = fn()
        e.cnt += 1
        ins.then_inc(e.sem, 1)
        self._snap(e, (e.sem, e.cnt))
        self._record((e.sem, e.cnt), reads, writes)

    def group(self, e, fns, reads=(), writes=()):
        self.deps(e, reads, writes)
        ins = None
        for fn in fns:
            ins = fn()
        e.cnt += 1
        ins.then_inc(e.sem, 1)
        self._snap(e, (e.sem, e.cnt))
        self._record((e.sem, e.cnt), reads, writes)

    def dma(self, e, ds, out, in_, reads=(), writes=(), **kw):
        self.deps(e, reads, writes)
        ins = e.h.dma_start(out=out, in_=in_, **kw)
        ds.cnt += 1
        ins.then_inc(ds.sem, 16)
        self._snap(e, (ds.sem, 16 * ds.cnt))
        self._record((ds.sem, 16 * ds.cnt), reads, writes)

    def barrier(self):
        for e in self.engs:
            for o in self.engs:
                if o is not e and o.cnt > 0:
                    self._wait(e, o.sem, o.cnt)


def build_nc(nblocks=NB, dbg=False):
    nc = bass.Bass("TRN2", target_bir_lowering=False)
    dt = nc.dram_tensor
    x_d = dt("x", [S, D], F32, kind="ExternalInput").ap()
    ng_d = dt("norm_g", [D], F32, kind="ExternalInput").ap()
    win_d = dt("w_in", [D, DIN], F32, kind="ExternalInput").ap()
    bg_d = dt("b_gate", [2 * D], F32, kind="ExternalInput").ap()
    rb_d = dt("rel_bias", [8, 257], F32, kind="ExternalInput").ap()
    lng_d = dt("sgu_ln_g", [512], F32, kind="ExternalInput").ap()
    lnb_d = dt("sgu_ln_b", [512], F32, kind="ExternalInput").ap()
    ws_d = dt("w_s", [4, 128, 128], F32, kind="ExternalInput").ap()
    bs_d = dt("b_s", [512], F32, kind="ExternalInput").ap()
    wpa_d = dt("w_pa", [512, D], F32, kind="ExternalInput").ap()
    wpb_d = dt("w_pb", [512, D], F32, kind="ExternalInput").ap()
    wout_d = dt("w_out", [D, D], F32, kind="ExternalInput").ap()
    fg_d = dt("final_g", [D], F32, kind="ExternalInput").ap()
    id_d = dt("ident", [128, 128], F32, kind="ExternalInput").ap()
    tr_d = dt("trilT", [128, 128], F32, kind="ExternalInput").ap()
    y_d = dt("y", [S, D], F32, kind="ExternalOutput").ap()
    ext_d = dt("ext_scr", [8, 384], F32).ap()
    t2_d = dt("toe_scr", [8, 128, 256], F32).ap()

    def sb(name, shape, dtype):
        return nc.alloc_sbuf_tensor(name, shape, dtype)

    def dsz(dtype):
        return 4 if dtype == F32 else 2

    W = sb("W", [128, 8, DIN], BF16)
    Wpa = sb("Wpa", [128, 4, D], BF16)
    Wpb = sb("Wpb", [128, 4, D], BF16)
    Wout = sb("Wout", [128, 8, D], BF16)
    NF = 3
    Fs = [sb(f"F{i}", [128, D], F32) for i in range(NF)]
    hb = [sb(f"hb{i}", [128, D], BF16) for i in range(2)]
    hT = [sb(f"hT{i}", [128, 8, BT], BF16) for i in range(2)]
    qT = sb("qT", [128, 4, BT], BF16)
    kT = sb("kT", [128, 4, 768], BF16)
    Vr = sb("Vr", [128, 6, 768], BF16)
    PT = [sb("PT0", [128, 2, 512], BF16)]
    pt0 = nc.sbuf_base - 2048
    PT += [sb(f"PT{i}", [128, 2, 512], BF16) for i in (1, 2)]
    NT1 = 4
    T1 = [sb(f"T1_{i}", [128, BT], F32) for i in range(NT1)]
    vf = sb("vf", [128, 512], F32)
    vsq = sb("vsq", [128, 512], F32)
    vn = [sb(f"vn{i}", [128, 512], BF16) for i in range(2)]
    reg0 = nc.sbuf_base
    merged = sb("merged", [128, 8, BT], BF16)
    reg0 = nc.sbuf_base - 8 * BT * 2
    Gt = sb("Gt", [128, 2, BT], F32)
    M1 = sb("M1", [128, 2, BT], F32)
    ya0 = sb("ya0", [128, 4, BT], BF16)
    yb0 = sb("yb0", [128, 4, BT], BF16)
    assert nc.sbuf_base - reg0 == 12288, (nc.sbuf_base, reg0)
    _off = [reg0]

    def sbat(name, shape, dtype):
        n = int(np.prod(shape[1:])) * dsz(dtype)
        t = nc.alloc_sbuf_tensor_at(name, shape, dtype, offset=_off[0])
        _off[0] += (n + 31) // 32 * 32
        assert _off[0] <= reg0 + 12288
        return t

    ident_f = sbat("ident_f", [128, 128], F32)
    prow = sbat("prow", [28, 128], F32)
    tril_f = sbat("tril_f", [128, 128], F32)
    ws_f = sbat("ws_f", [128, 4, 128], F32)
    ws_b = sbat("ws_b", [128, 4, 128], BF16)
    bsb = sbat("bsb", [128, 512], F32)
    lnb_bc = sbat("lnb_bc", [128, 512], F32)
    lnb_hi = sbat("lnb_hi", [128, 512], BF16)
    lnb_lo = sbat("lnb_lo", [128, 512], BF16)
    ya = [ya0, sb("ya1", [128, 4, BT], BF16)]
    yb = [yb0, sb("yb1", [128, 4, BT], BF16)]
    XA = [nc.alloc_sbuf_tensor_at(f"XA{i}", [128, D], F32, offset=pt0 + 4096 * i) for i in range(2)]
    XB = [nc.alloc_sbuf_tensor_at(f"XB{i}", [128, D], F32, offset=reg0 + 4096 * i) for i in range(2)]
    fgb = sb("fgb", [128, D], F32)
    Cg = sb("Cg", [128, 4, 128], F32)
    EBX = sb("EBX", [128, 8, 256], BF16)
    WT = sb("WT", [128, 4, 128], BF16)
    identb = sb("identb", [128, 128], BF16)
    ONES3 = sb("ONES3", [128, 192], BF16)
    ng = sb("ng", [128, 8], F32)
    bg = sb("bg", [128, 16], F32)
    hbg = sb("hbg", [128, 16], F32)
    lng = sb("lng", [128, 4], F32)
    ss = sb("ss", [128, 2], F32)
    sa2 = sb("sa2", [128, 2], F32)
    rstd = sb("rstd", [128, 2], F32)
    ss2 = sb("ss2", [128, 2], F32)
    sb2 = sb("sb2", [128, 2], F32)
    rstd2 = sb("rstd2", [128, 2], F32)
    s1 = sb("s1", [128, 2], F32)
    s2 = sb("s2", [128, 2], F32)
    mu = sb("mu", [128, 2], F32)
    msq = sb("msq", [128, 2], F32)
    var = sb("var", [128, 2], F32)
    rstdv = sb("rstdv", [128, 2], F32)
    rbs = sb("rbs", [8, 257], F32)
    es = sb("es", [8, 384], F32)
    nbias = sb("nbias", [8, 1], F32)
    epsc = sb("epsc", [128, 2], F32)

    NG = 8
    Db = [nc.alloc_psum_tensor(f"Db{i}", [128, 2, 512], F32) for i in range(4)]
    Gb = [Db[i // 2][:, i % 2, :] for i in range(NG)]

    sc = Sched(nc)
    pe, act, dve, pool, sp = sc.pe, sc.act, sc.dve, sc.pool, sc.sp
    R = Res
    Gres = [R(f"G{i}", True) for i in range(NG)]
    Fres = [R(f"F{i}") for i in range(NF)]
    Fd = [DmaSem(nc, f"d_F{i}") for i in range(NF)]
    XAres = [R(f"XA{i}") for i in range(2)]
    XBres = [R(f"XB{i}") for i in range(2)]
    XAd = [DmaSem(nc, f"d_XA{i}") for i in range(2)]
    XBd = [DmaSem(nc, f"d_XB{i}") for i in range(2)]
    setup_d = DmaSem(nc, "d_setup")
    chain_d = DmaSem(nc, "d_chain")
    cnt = {"g": 0, "f": 0, "t1": 0, "s": 0, "nd": 0, "cast": 0}

    pinned = set()

    def gnext(pin=False):
        for _ in range(NG):
            i = cnt["g"] % NG
            cnt["g"] += 1
            if i not in pinned:
                if pin:
                    pinned.add(i)
                return Gb[i], Gres[i]
        raise AssertionError("all generic PSUM banks pinned")

    def gunpin(res):
        pinned.discard(Gres.index(res))

    def snext():
        for _ in range(NG):
            i = cnt["g"] % NG
            if i % 2 == 1 or i in pinned or (i + 1) in pinned:
                cnt["g"] += 1
                continue
            cnt["g"] += 2
            return Db[i // 2], [Gres[i], Gres[i + 1]]
        raise AssertionError("no free PSUM bank pair")

    def fnext():
        i = cnt["f"] % NF
        cnt["f"] += 1
        return Fs[i], Fres[i], Fd[i]

    T1res = [R(f"T1_{i}") for i in range(NT1)]

    def t1next():
        i = cnt["t1"] % NT1
        cnt["t1"] += 1
        return T1[i], T1res[i]

    mm = nc.tensor.matmul

    def bcast_mid(a, n):
        pat = [list(p) for p in a.ap]
        return bass.AP(a.tensor, a.offset, [pat[0], [0, n]] + pat[1:])

    r_setup = R("setup_in")

    def sdma(out, in_, **kw):
        sc.dma(sp, setup_d, out, in_, reads=(), writes=(r_setup,), **kw)

    sdma(ident_f[:], id_d[:, :])
    sdma(tril_f[:], tr_d[:, :])
    sdma(ws_f[:], ws_d.rearrange("g t s -> t g s"))
    sdma(bsb[:], bs_d.partition_broadcast(128))
    sdma(lnb_bc[:], lnb_d.partition_broadcast(128))
    sdma(fgb[:], fg_d.partition_broadcast(128))
    sdma(prow[0:8, :], ng_d.rearrange("(r p) -> r p", p=128))
    sdma(prow[8:24, :], bg_d.rearrange("(r p) -> r p", p=128))
    sdma(prow[24:28, :], lng_d.rearrange("(r p) -> r p", p=128))
    sdma(rbs[:], rb_d[:, :])
    r_setup.w = (setup_d.sem, 16 * setup_d.cnt)

    r_c = {k: R(k) for k in ["identb", "hbg", "ws_b", "WT", "lnb_hi", "lnb_lo", "Cg", "nb", "es", "ext", "t2",
                             "EBX", "ONES3", "Vr0", "epsc"]}
    sc.op(pool, lambda: nc.gpsimd.memset(epsc[:, 0:1], EPS), [], [r_c["epsc"]])
    sc.op(pool, lambda: nc.gpsimd.memset(epsc[:, 1:2], 4.0 * EPS), [r_c["epsc"]], [r_c["epsc"]])
    sc.op(dve, lambda: nc.vector.tensor_copy(out=identb[:], in_=ident_f[:]), [r_setup], [r_c["identb"]])
    gb, gr = gnext()
    sc.group(pe, [lambda: mm(gb[:, 0:28], prow[0:28, :], ident_f[0:28, 0:28], start=True, stop=True)], [r_setup], [gr])
    r_c["pcols"] = R("pcols")
    sc.op(dve, lambda: nc.vector.tensor_copy(out=ng[:], in_=gb[:, 0:8]), [gr], [r_c["pcols"]])
    sc.op(dve, lambda: nc.vector.tensor_copy(out=lng[:], in_=gb[:, 24:28]), [gr], [r_c["pcols"]])
    sc.op(dve, lambda: nc.vector.tensor_scalar(out=hbg[:], in0=gb[:, 8:24], scalar1=0.5, scalar2=None, op0=ALU.mult),
          [gr], [r_c["hbg"]])
    sc.op(act, lambda: nc.scalar.copy(out=ws_b[:], in_=ws_f[:]), [r_setup], [r_c["ws_b"]])
    gb, gr = gnext()
    gbv = gb[:].bitcast(BF16).rearrange("p (a b) -> p a b", a=8)
    sc.group(pe, [(lambda g=g: nc.tensor.transpose(out=gbv[:, g, :], in_=ws_b[:, g, :], identity=identb[:]))
                  for g in range(4)], [r_c["ws_b"], r_c["identb"]], [gr])
    sc.op(dve, lambda: nc.vector.tensor_tensor(out=WT[:], in0=gbv[:, 0:4, :], in1=bcast_mid(tril_f[:], 4), op=ALU.mult),
          [gr, r_setup], [r_c["WT"]])
    sc.op(dve, lambda: nc.vector.tensor_copy(out=lnb_hi[:], in_=lnb_bc[:]), [r_setup], [r_c["lnb_hi"]])
    sc.op(dve, lambda: nc.vector.tensor_tensor(out=lnb_lo[:], in0=lnb_bc[:], in1=lnb_hi[:], op=ALU.subtract),
          [r_setup, r_c["lnb_hi"]], [r_c["lnb_lo"]])
    gb, gr = gnext()
    gcv = gb[:].rearrange("p (a b) -> p a b", a=4)
    fns = []
    for g in range(4):
        fns.append(lambda g=g: mm(gcv[:, g, :], lnb_hi[:, g * 128:(g + 1) * 128], WT[:, g, :], start=True, stop=False))
        fns.append(lambda g=g: mm(gcv[:, g, :], lnb_lo[:, g * 128:(g + 1) * 128], WT[:, g, :], start=False, stop=True))
    sc.group(pe, fns, [r_c["lnb_hi"], r_c["lnb_lo"], r_c["WT"]], [gr])
    sc.op(dve, lambda: nc.vector.tensor_tensor(out=Cg[:], in0=gcv[:, :, :],
                                               in1=bsb[:].rearrange("p (a b) -> p a b", a=4), op=ALU.add),
          [gr, r_setup], [r_c["Cg"]])
    sc.op(dve, lambda: nc.vector.tensor_scalar(out=nbias[:], in0=rbs[:, 256:257], scalar1=-1.0, scalar2=None,
                                               op0=ALU.mult), [r_setup], [r_c["nb"]])
    sc.op(pool, lambda: nc.gpsimd.memset(es[:], 1.0), [], [r_c["es"]])
    sc.op(act, lambda: nc.scalar.activation(out=es[:, 0:257], in_=rbs[:], func=AF.Exp, bias=nbias[:, 0:1], scale=1.0),
          [r_setup, r_c["nb"], r_c["es"]], [r_c["es"]])
    sc.dma(sp, chain_d, ext_d[:, :], es[:], [r_c["es"]], [r_c["ext"]])
    sc.dma(sp, chain_d, t2_d[:, :, :], bass.AP(ext_d.tensor, 128, [[384, 8], [-1, 128], [1, 256]]),
           [r_c["ext"]], [r_c["t2"]])
    for half in range(2):
        ft, fr, fd = fnext()
        sc.dma(sp, fd, ft[:].rearrange("p (a b) -> p a b", a=4),
               t2_d[4 * half:4 * half + 4, :, :].rearrange("h k j -> k h j"), [r_c["t2"]], [fr])
        sc.op(dve, lambda ft=ft, half=half: nc.vector.tensor_copy(
            out=EBX[:, 4 * half:4 * half + 4, 0:256], in_=ft[:].rearrange("p (a b) -> p a b", a=4)),
            [fr], [r_c["EBX"]])
    sc.op(pool, lambda: nc.gpsimd.memset(EBX[64:128, :, 0:64], 0.0), [r_c["EBX"]], [r_c["EBX"]])
    sc.op(pool, lambda: nc.gpsimd.memset(ONES3[:], 0.0), [], [r_c["ONES3"]])
    sc.op(pool, lambda: nc.gpsimd.memset(ONES3[:, 64:128], 1.0), [r_c["ONES3"]], [r_c["ONES3"]])
    sc.op(pool, lambda: nc.gpsimd.memset(Vr[:], 0.0), [], [r_c["Vr0"]])
    sc.barrier()

    Wres = {}
    wjobs = []
    for p in range(6):
        c0 = 1024 * p
        c1 = min(c0 + 1024, DIN)
        for kc in range(8):
            wjobs.append(("in", p, kc, win_d[kc * 128:(kc + 1) * 128, c0:c1], W[:, kc, c0:c1], c1 - c0, ng[:, kc:kc + 1]))
    for kc in range(4):
        wjobs.append(("pa", 0, kc, wpa_d[kc * 128:(kc + 1) * 128, :], Wpa[:, kc, :], D, 0.5))
    for kc in range(4):
        wjobs.append(("pb", 0, kc, wpb_d[kc * 128:(kc + 1) * 128, :], Wpb[:, kc, :], D, 0.25))
    for kc in range(8):
        wjobs.append(("out", 0, kc, wout_d[kc * 128:(kc + 1) * 128, :], Wout[:, kc, :], D, 0.5))

    wpool = {"slots": None, "i": 0}

    def set_wpool(kind):
        base = [(Fs[i], Fres[i], Fd[i]) for i in range(NF)]
        xa = [(XA[i], XAres[i], XAd[i]) for i in range(2)]
        xb = [(XB[i], XBres[i], XBd[i]) for i in range(2)]
        wpool["slots"] = {"all": base + xa + xb, "late": base + xb, "base": base}[kind]

    def fence(ress):
        for e in (pe, act, dve, pool):
            sc.deps(e, [], ress)

    def emit_wjob(job):
        name, p, kc, src, dst, n, scal = job
        ft, fr, fd = wpool["slots"][wpool["i"] % len(wpool["slots"])]
        wpool["i"] += 1
        sc.dma(sp, fd, ft[:, 0:n], src, [], [fr])
        res = R(f"W{name}{p}_{kc}")
        Wres[(name, p, kc)] = res
        k = cnt["cast"]
        cnt["cast"] += 1
        if k % 2 == 0:
            sc.op(act, lambda: nc.scalar.mul(out=dst, in_=ft[:, 0:n], mul=scal), [fr, r_setup], [res])
        else:
            sc.op(dve, lambda: nc.vector.tensor_scalar(out=dst, in0=ft[:, 0:n], scalar1=scal, scalar2=None, op0=ALU.mult),
                  [fr, r_setup], [res])

    def wr_in(c0, n=128):
        p0, p1 = c0 // 1024, (c0 + n - 1) // 1024
        return [Wres[("in", p, kc)] for p in range(p0, p1 + 1) for kc in range(8)]

    r_hb = [R("hb0"), R("hb1")]
    r_junk = R("junk")
    r_ss, r_sa2, r_rstd = R("ss"), R("sa2"), R("rstd")
    r_hT = [[R(f"hT{p}_{i}") for i in range(2)] for p in range(2)]
    r_q = [R(f"q{j}") for j in range(4)]
    r_k = [[R(f"k{s}_{j}") for j in range(4)] for s in range(3)]
    r_V = [R(f"V{s}") for s in range(6)]
    r_PT = [R(f"PT{u}") for u in range(3)]
    r_vf, r_vsq = R("vf"), R("vsq")
    r_vn = [R("vn0"), R("vn1")]
    r_s1, r_s2, r_mu, r_msq, r_var, r_rstdv = R("s1"), R("s2"), R("mu"), R("msq"), R("var"), R("rstdv")
    r_ya = [[R(f"ya{p}_{j}") for j in range(4)] for p in range(2)]
    r_yb = [[R(f"yb{p}_{g}") for g in range(4)] for p in range(2)]
    r_Gt, r_M1 = R("Gt"), R("M1")
    r_m = [R(f"m{j}") for j in range(8)]
    r_ss2, r_sb2, r_rstd2 = R("ss2"), R("sb2"), R("rstd2")
    xin = {}

    def normA(b):
        for i in range(2):
            ti = 2 * b + i
            ft, fr, fd = fnext()
            sc.dma(sp, fd, ft[:], x_d[ti * 128:(ti + 1) * 128, :], [], [fr])
            xin[ti] = (ft, fr)
            sc.op(act, lambda ft=ft, i=i: nc.scalar.activation(out=hb[i][:], in_=ft[:], func=AF.Square,
                                                              accum_out=ss[:, i:i + 1]),
                  [fr], [r_hb[i], r_ss])
        yield
        sc.op(act, lambda: nc.scalar.activation(out=sa2[:], in_=ss[:], func=AF.Sqrt, scale=1.0 / D, bias=epsc[:, 0:1]),
              [r_ss, r_c["epsc"]], [r_sa2])
        yield
        sc.op(dve, lambda: nc.vector.reciprocal(out=rstd[:], in_=sa2[:]), [r_sa2], [r_rstd])
        yield
        for i in range(2):
            ft, fr = xin[2 * b + i]
            sc.op(act, lambda ft=ft, i=i: nc.scalar.mul(out=hb[i][:], in_=ft[:], mul=rstd[:, i:i + 1]),
                  [fr, r_rstd], [r_hb[i]])
        yield

    def trans(b):
        for i in range(2):
            for half in range(2):
                gb, gr = gnext()
                gv = gb[:].rearrange("p (a b) -> p a b", a=4)
                sc.group(pe, [(lambda c=c, gv=gv: mm(gv[:, c, :], hb[i][:, (4 * half + c) * 128:(4 * half + c + 1) * 128],
                                                     identb[:], start=True, stop=True)) for c in range(4)],
                         [r_hb[i]], [gr])
                if half == 0:
                    sc.op(dve, lambda gv=gv: nc.vector.tensor_copy(out=hT[b % 2][:, 0:4, i * 128:(i + 1) * 128], in_=gv[:, :, :]),
                          [gr], [r_hT[b % 2][i]])
                else:
                    sc.op(act, lambda gv=gv: nc.scalar.copy(out=hT[b % 2][:, 4:8, i * 128:(i + 1) * 128], in_=gv[:, :, :]),
                          [gr], [r_hT[b % 2][i]])
                yield

    def proj_fm(b, out_ap, c0):
        return [(lambda kc=kc: mm(out_ap, W[:, kc, c0:c0 + 128], hT[b % 2][:, kc, :], start=(kc == 0), stop=(kc == 7)))
                for kc in range(8)]

    def projB(b):
        seg = b % 3
        for i in range(2):
            gb, gr = gnext(pin=True)
            sc.group(pe, [(lambda kc=kc, gb=gb, i=i: mm(gb[:, :], hT[b % 2][:, kc, i * 128:(i + 1) * 128], W[:, kc, C_VB:C_VB + 512],
                                                         start=(kc == 0), stop=(kc == 7))) for kc in range(8)],
                     [r_hT[b % 2][i]] + wr_in(C_VB, 512), [gr])
            sc.op(act, lambda gb=gb: nc.scalar.activation(out=vsq[:], in_=gb[:, :], func=AF.Square, scale=GELU_A ** 0.5),
                  [gr], [r_vsq])
            yield
            sc.op(dve, lambda gb=gb: nc.vector.scalar_tensor_tensor(out=vsq[:], in0=vsq[:], scalar=1.0, in1=gb[:, :],
                                                                    op0=ALU.add, op1=ALU.mult), [gr, r_vsq], [r_vsq])
            sc.op(act, lambda: nc.scalar.activation(out=vsq[:], in_=vsq[:], func=AF.Tanh, scale=GELU_C), [r_vsq], [r_vsq])
            yield
            sc.op(dve, lambda gb=gb: nc.vector.scalar_tensor_tensor(out=vf[:], in0=vsq[:], scalar=1.0, in1=gb[:, :],
                                                                    op0=ALU.add, op1=ALU.mult), [gr, r_vsq], [r_vf])
            gunpin(gr)
            sc.op(dve, lambda: nc.vector.tensor_reduce(out=s1[:, 0:1], in_=vf[:], axis=mybir.AxisListType.X, op=ALU.add),
                  [r_vf], [r_s1])
            sc.op(act, lambda: nc.scalar.activation(out=vsq[:], in_=vf[:], func=AF.Square, accum_out=s2[:, 0:1]),
                  [r_vf, r_vsq], [r_vsq, r_s2])
            yield
            sc.op(dve, lambda: nc.vector.tensor_scalar(out=mu[:, 0:1], in0=s1[:, 0:1], scalar1=1.0 / 512, scalar2=None,
                                                       op0=ALU.mult), [r_s1], [r_mu])
            sc.op(dve, lambda: nc.vector.scalar_tensor_tensor(out=msq[:, 0:1], in0=s1[:, 0:1], scalar=-1.0 / (512.0 * 512.0),
                                                              in1=s1[:, 0:1], op0=ALU.mult, op1=ALU.mult), [r_s1], [r_msq])
            sc.op(dve, lambda: nc.vector.scalar_tensor_tensor(out=var[:, 0:1], in0=s2[:, 0:1], scalar=1.0 / 512,
                                                              in1=msq[:, 0:1], op0=ALU.mult, op1=ALU.add),
                  [r_s2, r_msq], [r_var])
            sc.op(act, lambda: nc.scalar.activation(out=var[:, 0:1], in_=var[:, 0:1], func=AF.Sqrt, scale=1.0,
                                                    bias=epsc[:, 1:2]), [r_var, r_c["epsc"]], [r_var])
            yield
            sc.op(dve, lambda: nc.vector.reciprocal(out=rstdv[:, 0:1], in_=var[:, 0:1]), [r_var], [r_rstdv])
            sc.op(dve, lambda i=i: nc.vector.tensor_scalar(out=vn[i][:], in0=vf[:], scalar1=mu[:, 0:1],
                                                           scalar2=rstdv[:, 0:1], op0=ALU.subtract, op1=ALU.mult),
                  [r_vf, r_mu, r_rstdv], [r_vn[i]])
            yield
        for j in range(4):
            gb, gr = gnext()
            gv = gb[:].rearrange("p (a b) -> p a b", a=2)
            sc.group(pe, proj_fm(b, gv[:, 0, :], C_Q + j * 128) + proj_fm(b, gv[:, 1, :], C_K + j * 128),
                     r_hT[b % 2] + wr_in(C_Q + j * 128) + wr_in(C_K + j * 128), [gr])
            sc.op(act, lambda gv=gv, j=j: nc.scalar.copy(out=qT[:, j, :], in_=gv[:, 0, :]), [gr], [r_q[j]])
            sc.op(dve, lambda gv=gv, j=j: nc.vector.tensor_copy(out=kT[:, j, seg * 256:(seg + 1) * 256], in_=gv[:, 1, :]),
                  [gr], [r_k[seg][j]])
            yield
        for i in range(2):
            ti = 2 * b + i
            slot = ti % 6
            gb, gr = gnext()
            sc.group(pe, [(lambda kc=kc, gb=gb, i=i: mm(gb[:, :], hT[b % 2][:, kc, i * 128:(i + 1) * 128], W[:, kc, C_V:C_V + 512],
                                                         start=(kc == 0), stop=(kc == 7))) for kc in range(8)],
                     [r_hT[b % 2][i]] + wr_in(C_V, 512), [gr])
            vdst = Vr[:, slot, :].rearrange("p (j c) -> p j c", j=4)
            sc.op(act, lambda gb=gb, vdst=vdst: nc.scalar.copy(
                out=vdst[:, :, 0:64], in_=gb[:, :].rearrange("p (j c) -> p j c", j=4)[:, :, 0:64]), [gr], [r_V[slot]])
            sc.op(dve, lambda gb=gb, vdst=vdst: nc.vector.tensor_copy(
                out=vdst[:, :, 128:192], in_=gb[:, :].rearrange("p (j c) -> p j c", j=4)[:, :, 64:128]), [gr], [r_V[slot]])
            yield
    TILES = {0: (0, 256, 0, 128), 1: (0, 0, 0, 256), 2: (1, 0, 0, 256), 3: (1, 256, 0, 256),
             4: (2, 0, 0, 256), 5: (2, 256, 128, 256)}
    UNIT_TILES = {0: (1, 0), 1: (2, 3), 2: (4, 5)}
    UNIT_W = {0: 384, 1: 512, 2: 384}

    def attention(b):
        units = [u for u in (2, 1, 0) if 2 * b - 4 + UNIT_TILES[u][0] >= 0 and 2 * b - 4 + UNIT_TILES[u][1] >= 0]
        items = [(j, u) for j in range(4) for u in units]
        tgs = {}
        nd = {}

        def nd_banks(j):
            if j not in nd:
                nb_, nr_ = gnext(pin=True)
                db_, dr_ = gnext(pin=True)
                nd[j] = (nb_, nr_, db_, dr_)
            return nd[j]

        def gate(j):
            gb, gr = gnext()
            sc.group(pe, proj_fm(b, gb[:, 0:256], C_GA + j * 128), r_hT[b % 2] + wr_in(C_GA + j * 128), [gr])
            tg, tgr = t1next()
            sc.op(act, lambda: nc.scalar.activation(out=tg[:], in_=gb[:, 0:256], func=AF.Tanh, scale=0.5), [gr], [tgr])
            sc.op(dve, lambda: nc.vector.scalar_tensor_tensor(out=tg[:], in0=tg[:], scalar=1.0, in1=gb[:, 0:256],
                                                              op0=ALU.add, op1=ALU.mult), [gr, tgr], [tgr])
            tgs[j] = (tg, tgr)

        def scores(j, u):
            sbuf, sres2 = snext()
            fns = []
            rd = [r_q[j]]
            for t in UNIT_TILES[u]:
                _, uoff, qlo, qhi = TILES[t]
                gt = 2 * b - 4 + t
                kpos = (gt * 128) % 768
                rd.append(r_k[kpos // 256][j])
                for r in range(2):
                    fns.append(lambda r=r, uoff=uoff, qlo=qlo, qhi=qhi, kpos=kpos: mm(
                        sbuf[:, r, uoff:uoff + qhi - qlo], kT[64 * r:64 * r + 64, j, kpos:kpos + 128],
                        qT[64 * r:64 * r + 64, j, qlo:qhi], start=True, stop=True))
            sc.group(pe, fns, rd, sres2)
            w = UNIT_W[u]
            sc.op(act, lambda: nc.scalar.activation(out=PT[u][:, :, 0:w], in_=sbuf[:, :, 0:w], func=AF.Exp, scale=0.125),
                  sres2, [r_PT[u]])
            if u == 0:
                sc.op(dve, lambda: nc.vector.memset(PT[0][0:64, :, 320:384], 0.0), [r_PT[0]], [r_PT[0]])
                sc.op(dve, lambda: nc.vector.memset(PT[0][0:64, :, 192:256], 0.0), [r_PT[0]], [r_PT[0]])
            elif u == 1:
                sc.op(dve, lambda: nc.vector.tensor_tensor(out=PT[1][:, :, 256:384], in0=PT[1][:, :, 256:384],
                                                           in1=EBX[:, 2 * j:2 * j + 2, 128:256], op=ALU.mult),
                      [r_PT[1]], [r_PT[1]])
            else:
                sc.op(dve, lambda: nc.vector.tensor_tensor(out=PT[2][:, :, 0:256], in0=PT[2][:, :, 0:256],
                                                           in1=EBX[:, 2 * j:2 * j + 2, 0:256], op=ALU.mult),
                      [r_PT[2]], [r_PT[2]])
                sc.op(dve, lambda: nc.vector.tensor_tensor(out=PT[2][:, :, 256:384], in0=PT[2][:, :, 256:384],
                                                           in1=EBX[:, 2 * j:2 * j + 2, 0:128], op=ALU.mult),
                      [r_PT[2]], [r_PT[2]])

        def pv(j, u):
            numb, numr, denb, denr = nd_banks(j)
            first = (u == units[0])
            last = (u == units[-1])
            tiles = sorted(UNIT_TILES[u])
            fns = []
            n = 2 * len(tiles)
            k = 0
            for t in tiles:
                _, uoff, qlo, qhi = TILES[t]
                slot = (2 * b - 4 + t) % 6
                for r in range(2):
                    st = first and k == 0
                    sp_ = last and k == n - 1
                    lhs_v = Vr[:, slot, 192 * j + 64 * r:192 * j + 64 * r + 128]
                    lhs_1 = ONES3[:, 64:192] if r == 0 else ONES3[:, 0:128]
                    rhs = PT[u][:, r, uoff:uoff + qhi - qlo]
                    fns.append(lambda lhs_v=lhs_v, rhs=rhs, qlo=qlo, qhi=qhi, st=st, sp_=sp_: mm(
                        numb[:, qlo:qhi], lhs_v, rhs, start=st, stop=sp_))
                    fns.append(lambda lhs_1=lhs_1, rhs=rhs, qlo=qlo, qhi=qhi, st=st, sp_=sp_: mm(
                        denb[:, qlo:qhi], lhs_1, rhs, start=st, stop=sp_))
                    k += 1
            sc.group(pe, fns, [r_PT[u]] + [r_V[(2 * b - 4 + t) % 6] for t in tiles], [numr, denr])
            if last:
                tg, tgr = tgs[j]
                rd_, rdr = t1next()
                sc.op(dve, lambda: nc.vector.reciprocal(out=rd_[:], in_=denb[:, 0:256]), [denr], [rdr])
                sc.op(dve, lambda: nc.vector.tensor_tensor(out=rd_[:], in0=numb[:, 0:256], in1=rd_[:], op=ALU.mult),
                      [numr, rdr], [rdr])
                sc.op(pool, lambda: nc.gpsimd.tensor_tensor(out=ya[b % 2][:, j, :], in0=rd_[:], in1=tg[:], op=ALU.mult),
                      [rdr, tgr], [r_ya[b % 2][j]])
                gunpin(numr)
                gunpin(denr)

        assert units[0] == 2
        pipelined = len(units) >= 2
        prev = None
        for (j, u) in items:
            if u == units[0]:
                gate(j)
            scores(j, u)
            yield 0
            if not pipelined:
                pv(j, u)
                yield 3
                continue
            if prev is not None:
                pv(*prev)
                yield (3 if prev[1] == units[-1] else 1)
            prev = (j, u)
        if prev is not None:
            pv(*prev)
            yield 3

    def sgu(b):
        for g in range(4):
            gb, gr = gnext(pin=True)
            gv = gb[:].rearrange("p (a b) -> p a b", a=2)
            sc.group(pe, proj_fm(b, gv[:, 0, :], C_UB + g * 128) + proj_fm(b, gv[:, 1, :], C_GB + g * 128),
                     r_hT[b % 2] + wr_in(C_UB + g * 128) + wr_in(C_GB + g * 128), [gr])
            tu, tur = t1next()
            tb, tbr = t1next()
            sc.op(act, lambda: nc.scalar.activation(out=tu[:], in_=gv[:, 0, :], func=AF.Square, scale=GELU_A ** 0.5),
                  [gr], [tur])
            sc.op(act, lambda: nc.scalar.activation(out=tb[:], in_=gv[:, 1, :], func=AF.Tanh, scale=0.5), [gr], [tbr])
            yield
            sc.op(dve, lambda: nc.vector.scalar_tensor_tensor(out=tu[:], in0=tu[:], scalar=1.0, in1=gv[:, 0, :],
                                                              op0=ALU.add, op1=ALU.mult), [gr, tur], [tur])
            sc.op(act, lambda: nc.scalar.activation(out=tu[:], in_=tu[:], func=AF.Tanh, scale=GELU_C), [tur], [tur])
            yield
            sc.op(dve, lambda: nc.vector.scalar_tensor_tensor(out=tu[:], in0=tu[:], scalar=1.0, in1=gv[:, 0, :],
                                                              op0=ALU.add, op1=ALU.mult), [gr, tur], [tur])
            sc.op(dve, lambda: nc.vector.scalar_tensor_tensor(out=tb[:], in0=tb[:], scalar=1.0, in1=gv[:, 1, :],
                                                              op0=ALU.add, op1=ALU.mult), [gr, tbr], [tbr])
            gunpin(gr)
            sc.op(pool, lambda: nc.gpsimd.tensor_tensor(out=tu[:], in0=tu[:], in1=tb[:], op=ALU.mult), [tur, tbr], [tur])
            yield
            mb, mr = gnext()
            sc.group(pe, [(lambda i=i: mm(mb[:, i * 128:(i + 1) * 128], vn[i][:, g * 128:(g + 1) * 128], WT[:, g, :],
                                          start=True, stop=True)) for i in range(2)], r_vn, [mr])
            sc.op(dve, lambda: nc.vector.scalar_tensor_tensor(
                out=tb[:].rearrange("p (a b) -> p a b", a=2), in0=mb[:, 0:256].rearrange("p (a b) -> p a b", a=2),
                scalar=lng[:, g:g + 1], in1=bcast_mid(Cg[:, g, :], 2), op0=ALU.mult, op1=ALU.add),
                [mr, tbr, r_setup], [tbr])
            sc.op(pool, lambda: nc.gpsimd.tensor_tensor(out=yb[b % 2][:, g, :], in0=tb[:], in1=tu[:], op=ALU.mult),
                  [tbr, tur], [r_yb[b % 2][g]])
            yield

    def phaseD(b):
        for j in range(8):
            gb, gr = gnext()
            gv = gb[:].rearrange("p (a b) -> p a b", a=2)
            sc.group(pe, proj_fm(b, gv[:, 0, :], C_GTA + j * 128) + proj_fm(b, gv[:, 1, :], C_GTB + j * 128),
                     r_hT[b % 2] + wr_in(C_GTA + j * 128) + wr_in(C_GTB + j * 128), [gr])
            sc.op(act, lambda: nc.scalar.activation(out=Gt[:, 0, :], in_=gv[:, 0, :], func=AF.Tanh, bias=hbg[:, j:j + 1],
                                                    scale=0.5), [gr, r_c["hbg"]], [r_Gt])
            sc.op(act, lambda: nc.scalar.activation(out=Gt[:, 1, :], in_=gv[:, 1, :], func=AF.Tanh,
                                                    bias=hbg[:, 8 + j:9 + j], scale=0.5), [gr, r_c["hbg"]], [r_Gt])
            yield
            pb_, pr = gnext()
            pv = pb_[:].rearrange("p (a b) -> p a b", a=2)
            fns = [(lambda ec=ec: mm(pv[:, 0, :], Wpa[:, ec, j * 128:(j + 1) * 128], ya[b % 2][:, ec, :], start=(ec == 0),
                                     stop=(ec == 3))) for ec in range(4)]
            fns += [(lambda ec=ec: mm(pv[:, 1, :], Wpb[:, ec, j * 128:(j + 1) * 128], yb[b % 2][:, ec, :], start=(ec == 0),
                                      stop=(ec == 3))) for ec in range(4)]
            sc.group(pe, fns, r_ya[b % 2] + r_yb[b % 2] + [Wres[("pa", 0, ec)] for ec in range(4)] + [Wres[("pb", 0, ec)] for ec in range(4)],
                     [pr])
            sc.op(dve, lambda: nc.vector.scalar_tensor_tensor(out=M1[:], in0=Gt[:], scalar=1.0, in1=pv[:, :, :],
                                                              op0=ALU.add, op1=ALU.mult), [pr, r_Gt], [r_M1])
            sc.op(pool, lambda: nc.gpsimd.tensor_tensor(out=merged[:, j, :], in0=M1[:, 0, :], in1=M1[:, 1, :], op=ALU.add),
                  [r_M1], [r_m[j]])
            yield

    def phaseE(b):
        slots = []
        for i in range(2):
            ti = 2 * b + i
            ft, fr, fd = fnext()
            sc.dma(sp, fd, ft[:], x_d[ti * 128:(ti + 1) * 128, :], [], [fr])
            slots.append((ft, fr, fd))
            for hf in range(2):
                gb, gr = gnext()
                sc.group(pe, [(lambda dc=dc, gb=gb: mm(gb[:, :], merged[:, dc, i * 128:(i + 1) * 128],
                                                        Wout[:, dc, hf * 512:(hf + 1) * 512], start=(dc == 0), stop=(dc == 7)))
                              for dc in range(8)], r_m + [Wres[("out", 0, dc)] for dc in range(8)], [gr])
                sc.op(dve, lambda gb=gb, ft=ft, hf=hf: nc.vector.tensor_tensor(
                    out=ft[:, hf * 512:(hf + 1) * 512], in0=gb[:, :], in1=ft[:, hf * 512:(hf + 1) * 512], op=ALU.add),
                    [gr, fr], [fr])
                yield
            sc.op(act, lambda ft=ft, i=i: nc.scalar.activation(out=PT[0][:].rearrange("p a b -> p (a b)"), in_=ft[:], func=AF.Square,
                                                              accum_out=ss2[:, i:i + 1]), [fr], [r_PT[0], r_ss2])
        sc.op(act, lambda: nc.scalar.activation(out=sb2[:], in_=ss2[:], func=AF.Sqrt, scale=1.0 / D, bias=epsc[:, 0:1]),
              [r_ss2, r_c["epsc"]], [r_sb2])
        yield
        sc.op(dve, lambda: nc.vector.reciprocal(out=rstd2[:], in_=sb2[:]), [r_sb2], [r_rstd2])
        for i in range(2):
            ti = 2 * b + i
            ft, fr, fd = slots[i]
            sc.op(dve,
                  lambda ft=ft, i=i: nc.vector.scalar_tensor_tensor(
                      out=ft[:], in0=ft[:], scalar=rstd2[:, i:i + 1], in1=fgb[:], op0=ALU.mult, op1=ALU.mult),
                  [fr, r_rstd2, r_setup], [fr])
            sc.dma(sp, fd, y_d[ti * 128:(ti + 1) * 128, :], ft[:], [fr], [])
            fr.rs[fd.sem.name] = (fd.sem, 16 * fd.cnt)
            yield

    def drain(g):
        for _ in g:
            pass

    def chain(*gens):
        for g in gens:
            yield from g

    def interleave(ga, gb_):
        for hint in ga:
            for _ in range(1 if hint is None else hint):
                next(gb_, None)
        drain(gb_)

    def wgen(jobs):
        for job in jobs:
            emit_wjob(job)
            yield

    set_wpool("all")
    for job in wjobs[:24]:
        emit_wjob(job)
    drain(normA(0))
    drain(trans(0))
    if nblocks > 1:
        drain(normA(1))
    drain(projB(0))
    set_wpool("late")
    fence(XAres)
    interleave(chain(attention(0), sgu(0)), wgen(wjobs[24:]))
    fence(XBres)
    for b in range(nblocks):
        if b + 1 < nblocks:
            if b >= 1:
                interleave(chain(trans(b + 1), projB(b + 1)), phaseE(b - 1))
            else:
                drain(chain(trans(b + 1), projB(b + 1)))
            st3 = chain(phaseD(b), normA(b + 2)) if b + 2 < nblocks else phaseD(b)
            interleave(chain(attention(b + 1), sgu(b + 1)), st3)
        else:
            if b >= 1:
                drain(phaseE(b - 1))
            drain(chain(phaseD(b), phaseE(b)))
    for fd in Fd:
        sc._wait(sp, fd.sem, 16 * fd.cnt)
    nc._sched_stats = (sc.nwait, {e.name: e.cnt for e in sc.engs})
    return nc


_CONST = {}


def _consts():
    if not _CONST:
        _CONST["ident"] = np.eye(128, dtype=np.float32)
        s = np.arange(128)[:, None]
        t = np.arange(128)[None, :]
        _CONST["trilT"] = (s <= t).astype(np.float32)
    return _CONST


def kernel(x, norm_g, w_in, b_gate, rel_bias, sgu_ln_g, sgu_ln_b, w_s, b_s, w_pa, w_pb, w_out, final_g):
    nblocks = NB
    f = lambda a: np.ascontiguousarray(np.asarray(a, dtype=np.float32))
    x = f(x)
    c = _consts()
    shared = {
        "norm_g": f(norm_g)[0], "w_in": f(w_in)[0], "b_gate": f(b_gate)[0], "rel_bias": f(rel_bias)[0],
        "sgu_ln_g": f(sgu_ln_g)[0], "sgu_ln_b": f(sgu_ln_b)[0], "w_s": f(w_s)[0], "b_s": f(b_s)[0].reshape(512),
        "w_pa": f(w_pa)[0], "w_pb": f(w_pb)[0], "w_out": f(w_out)[0], "final_g": f(final_g),
        "ident": c["ident"], "trilT": c["trilT"],
    }
    nc = build_nc(nblocks)
    in_maps = [dict(shared, x=x[i]) for i in range(8)]
    res = run_bass_kernel_spmd(nc, in_maps, core_ids=list(range(8)))
    return np.stack([np.asarray(r["y"], dtype=np.float32) for r in res.results], axis=0)
```
